# Optimizing a Trainium2 kernel written in Bass

```python
import math, functools
import jax, jax.numpy as jnp
from jax import lax
import numpy as np

D_MODEL = 1024
BATCH = 2
SEQ = 8192
DEPTH = 2
DEC_BATCH = 128
DEC_SEQ = 1
PAST_LEN = 2048
PAGE_SIZE = 128

N_EVEN = (DEPTH + 1) // 2
N_ODD = DEPTH // 2
MIX_W = D_MODEL // 2
H_A = 4
DH_A = MIX_W // (2 * H_A)
H_B = 4
DK_B = MIX_W // H_B
DV_B = MIX_W // H_B
H_C = 4
DK_C = MIX_W // H_C
DV_C = MIX_W // H_C
H_D = 4
DK_D = MIX_W // H_D
DV_D = MIX_W // H_D
H_X = 4
DH_X = D_MODEL // H_X
N_MEM = 256
D_FF = ((8 * D_MODEL // 3 + 255) // 256) * 256
CONV_W = 3
CHUNK = 64
Q_BLOCK = 128
ROPE_THETA = 10000.0
NORM_EPS = 1e-5
DN_ALPHA = (2.0 * DEPTH) ** 0.25
DN_BETA = (8.0 * DEPTH) ** -0.25
EV_IN = 7 * MIX_W
OD_IN = 8 * MIX_W + 2 * H_D

kernel_name = "hybrid_diffattn_hgrn2_retnet_mlstm_step"

F32 = jnp.float32


def _split(a, sizes):
    cuts = [int(c) for c in np.cumsum(sizes)[:-1]]
    return jnp.split(a, cuts, axis=-1)


def _heads(a, h):
    B, T, _ = a.shape
    return a.reshape(B, T, h, -1).transpose(0, 2, 1, 3)


def _merge(a):
    B, h, T, d = a.shape
    return a.transpose(0, 2, 1, 3).reshape(B, T, h * d)


def _layernorm(x, g, b):
    xf = x.astype(F32)
    mu = jnp.mean(xf, -1, keepdims=True)
    var = jnp.mean(jnp.square(xf - mu), -1, keepdims=True)
    return ((xf - mu) * lax.rsqrt(var + NORM_EPS) * g.astype(F32) + b.astype(F32)).astype(x.dtype)


def _rmsnorm(x, g):
    xf = x.astype(F32)
    return (xf * lax.rsqrt(jnp.mean(jnp.square(xf), -1, keepdims=True) + NORM_EPS) * g.astype(F32)).astype(x.dtype)


def _groupnorm(x, g):
    xf = x.astype(F32)
    mu = jnp.mean(xf, -1, keepdims=True)
    var = jnp.mean(jnp.square(xf - mu), -1, keepdims=True)
    return ((xf - mu) * lax.rsqrt(var + NORM_EPS) * g.astype(F32)).astype(x.dtype)


def _rope(x, pos):
    d = x.shape[-1]
    inv = ROPE_THETA ** (-jnp.arange(0, d // 2, dtype=F32) * 2.0 / d)
    ang = pos.astype(F32)[:, None] * inv[None, :]
    ang = ang.reshape((ang.shape[0],) + (1,) * (x.ndim - 3) + (d // 2,))
    cos, sin = jnp.cos(ang), jnp.sin(ang)
    x1 = x[..., : d // 2].astype(F32)
    x2 = x[..., d // 2:].astype(F32)
    return jnp.concatenate([x1 * cos - x2 * sin, x2 * cos + x1 * sin], -1).astype(x.dtype)


def _gated_linear_attention(q, k, v, log_f, s0):
    B, H, T, dk = q.shape
    dv = v.shape[-1]
    L = math.gcd(T, CHUNK)
    nc = T // L

    def blocks(a):
        return jnp.moveaxis(a.astype(F32).reshape(B, H, nc, L, a.shape[-1]), 2, 0)

    qc, kc, vc, gc = blocks(q), blocks(k), blocks(v), blocks(log_f)
    b = jnp.cumsum(gc, axis=3)
    b_end = b[:, :, :, -1:, :]
    q_in = qc * jnp.exp(b)
    k_in = kc * jnp.exp(-b)
    k_end = kc * jnp.exp(b_end - b)
    causal = jnp.tril(jnp.ones((L, L), bool))
    attn = jnp.where(causal, jnp.einsum('nbhld,nbhsd->nbhls', q_in, k_in), 0.0)
    o_intra = jnp.einsum('nbhls,nbhse->nbhle', attn, vc)

    def step(S, inp):
        q_i, k_e, v_i, dec = inp
        o_inter = jnp.einsum('bhld,bhde->bhle', q_i, S)
        S = dec[:, :, 0, :, None] * S + jnp.einsum('bhld,bhle->bhde', k_e, v_i)
        return S, o_inter

    s_T, o_inter = lax.scan(step, s0.astype(F32), (q_in, k_end, vc, jnp.exp(b_end)))
    o = jnp.moveaxis(o_intra + o_inter, 0, 2).reshape(B, H, T, dv)
    return o, s_T


def _mlstm(q, k, v, log_i, log_f, c0, n0, m0):
    B, H, T, dk = q.shape
    dv = v.shape[-1]
    L = math.gcd(T, CHUNK)
    nc = T // L

    def blocks(a):
        return jnp.moveaxis(a.astype(F32).reshape((B, H, nc, L) + a.shape[3:]), 2, 0)

    qc, kc, vc, ic, fc = blocks(q), blocks(k), blocks(v), blocks(log_i), blocks(log_f)
    causal = jnp.tril(jnp.ones((L, L), bool))

    def step(carry, inp):
        C, n, m = carry
        q_i, k_i, v_i, li, lf = inp
        b = jnp.cumsum(lf, -1)
        Dm = jnp.where(causal, b[..., :, None] - b[..., None, :] + li[..., None, :], -jnp.inf)
        m_t = jnp.maximum(b + m[..., None], jnp.max(Dm, -1))
        inter = jnp.exp(b + m[..., None] - m_t)
        W = jnp.exp(Dm - m_t[..., None]) * jnp.einsum('bhld,bhsd->bhls', q_i, k_i)
        num = inter[..., None] * jnp.einsum('bhld,bhde->bhle', q_i, C) + jnp.einsum('bhls,bhse->bhle', W, v_i)
        den = inter * jnp.einsum('bhld,bhd->bhl', q_i, n) + jnp.sum(W, -1)
        h = num / jnp.maximum(jnp.abs(den), jnp.exp(-m_t))[..., None]
        m_new = m_t[..., -1]
        c_scale = jnp.exp(b[..., -1] + m - m_new)
        w = jnp.exp(b[..., -1:] - b + li - m_new[..., None])
        C = c_scale[..., None, None] * C + jnp.einsum('bhl,bhld,bhle->bhde', w, k_i, v_i)
        n = c_scale[..., None] * n + jnp.einsum('bhl,bhld->bhd', w, k_i)
        return (C, n, m_new), h

    (C, n, m), h = lax.scan(step, (c0.astype(F32), n0.astype(F32), m0.astype(F32)), (qc, kc, vc, ic, fc))
    h = jnp.moveaxis(h, 0, 2).reshape(B, H, T, dv)
    return h, C, n, m


def _diff_attn(q, k, v, q_pos, k_pos, lam):
    s = jnp.einsum('bqhcd,bkhcd->bhcqk', q, k).astype(F32) * (DH_A ** -0.5)
    s = jnp.where(q_pos[:, None] >= k_pos[None, :], s, -jnp.inf)
    p = jax.nn.softmax(s, axis=-1)
    a = p[:, :, 0] - lam * p[:, :, 1]
    return jnp.einsum('bhqk,bkhe->bqhe', a.astype(v.dtype), v)


def _attend_prompt(q, k, v, lam):
    B, T = q.shape[0], q.shape[1]
    qb_len = math.gcd(T, Q_BLOCK)
    k_pos = jnp.arange(T)

    def block(i):
        start = i * qb_len
        qb = lax.dynamic_slice_in_dim(q, start, qb_len, axis=1)
        return _diff_attn(qb, k, v, start + jnp.arange(qb_len), k_pos, lam)

    o = lax.map(block, jnp.arange(T // qb_len))
    return jnp.moveaxis(o, 0, 1).reshape(B, T, H_A, 2 * DH_A)


def _attend_sample(q, k, v, lam, k_pool, v_pool, page_table):
    DB, Tn = q.shape[0], q.shape[1]
    k_past = k_pool[page_table].reshape(DB, -1, H_A, 2, DH_A)
    v_past = v_pool[page_table].reshape(DB, -1, H_A, 2 * DH_A)
    P = k_past.shape[1]
    k_all = jnp.concatenate([k_past.astype(k.dtype), k], 1)
    v_all = jnp.concatenate([v_past.astype(v.dtype), v], 1)
    return _diff_attn(q, k_all, v_all, P + jnp.arange(Tn), jnp.arange(P + Tn), lam)


def _even_mixer(x, pos, s0, w_in, w_out, lam_p, lam_init, subln_g, lb, b_norm_g, attend):
    B, T, _ = x.shape
    a_q, a_k, a_v, b_q, b_f, b_i, b_g = _split(x @ w_in, [MIX_W] * 7)
    q = _rope(a_q.reshape(B, T, H_A, 2, DH_A), pos)
    k = _rope(a_k.reshape(B, T, H_A, 2, DH_A), pos)
    v = a_v.reshape(B, T, H_A, 2 * DH_A)
    lp = lam_p.astype(F32)
    lam = jnp.exp(jnp.sum(lp[0] * lp[1])) - jnp.exp(jnp.sum(lp[2] * lp[3])) + lam_init
    o_a = _rmsnorm(attend(q, k, v, lam), subln_g) * (1.0 - lam_init)
    o_a = o_a.reshape(B, T, MIX_W).astype(x.dtype)
    f = lb + (1.0 - lb) * jax.nn.sigmoid(b_f.astype(F32))
    o_b, s_T = _gated_linear_attention(_heads(jax.nn.silu(b_q.astype(F32)), H_B), _heads(1.0 - f, H_B),
                                       _heads(b_i, H_B), _heads(jnp.log(f), H_B), s0)
    o_b = (_merge(_rmsnorm(o_b, b_norm_g)) * jax.nn.silu(b_g.astype(F32))).astype(x.dtype)
    y = jnp.concatenate([o_a, o_b], -1) @ w_out
    return y, k, v, s_T


def _odd_mixer(x, pos, sc0, dc0, dn0, dm0, w_in, b_if, w_out, c_norm_g, d_norm_g):
    B, T, _ = x.shape
    c_q, c_k, c_v, c_g, d_q, d_k, d_v, d_o, d_if = _split(x @ w_in, [MIX_W] * 8 + [2 * H_D])
    q = _heads(_rope(c_q.reshape(B, T, H_C, DK_C), pos).reshape(B, T, MIX_W), H_C)
    k = _heads(_rope(c_k.reshape(B, T, H_C, DK_C), pos).reshape(B, T, MIX_W), H_C) * (DK_C ** -0.5)
    log_gamma = jnp.log(1.0 - 2.0 ** (-5.0 - jnp.arange(H_C, dtype=F32)))
    lf = jnp.broadcast_to(log_gamma[None, :, None, None], (B, H_C, T, DK_C))
    o_c, sc = _gated_linear_attention(q, k, _heads(c_v, H_C), lf, sc0)
    o_c = (_merge(_groupnorm(o_c, c_norm_g)) * jax.nn.silu(c_g.astype(F32))).astype(x.dtype)
    gates = d_if.astype(F32) + b_if.astype(F32)
    log_i = gates[..., :H_D].transpose(0, 2, 1)
    log_f = jax.nn.log_sigmoid(gates[..., H_D:]).transpose(0, 2, 1)
    h, dc, dn, dm = _mlstm(_heads(d_q, H_D) * (DK_D ** -0.5), _heads(d_k, H_D), _heads(d_v, H_D),
                           log_i, log_f, dc0, dn0, dm0)
    o_d = (_merge(_groupnorm(h, d_norm_g)) * jax.nn.sigmoid(d_o.astype(F32))).astype(x.dtype)
    y = jnp.concatenate([o_c, o_d], -1) @ w_out
    return y, sc, dc, dn, dm


def _mem_kv(mem, w_kv):
    B, M, _ = mem.shape
    mk, mv = _split(mem @ w_kv, [D_MODEL, D_MODEL])
    return mk.reshape(B, M, H_X, DH_X), mv.reshape(B, M, H_X, DH_X)


def _cross_attn(x, mk, mv, wq, wo):
    B, T, _ = x.shape
    q = (x @ wq).reshape(B, T, H_X, DH_X)
    s = jnp.einsum('bqhd,bkhd->bhqk', q, mk.astype(q.dtype)).astype(F32) * (DH_X ** -0.5)
    p = jax.nn.softmax(s, axis=-1)
    o = jnp.einsum('bhqk,bkhd->bqhd', p.astype(x.dtype), mv.astype(x.dtype)).reshape(B, T, D_MODEL)
    return o @ wo


def _conv_ffn(x, buf, w_up, conv_w, conv_b, w_down):
    T = x.shape[1]
    up, gate = _split(x @ w_up, [D_FF, D_FF])
    cat = jnp.concatenate([buf.astype(up.dtype), up], 1)
    conv = conv_b
    for j in range(CONV_W):
        conv = conv + conv_w[j] * cat[:, j:j + T]
    y = (jax.nn.gelu(conv) * gate) @ w_down
    return y, cat[:, cat.shape[1] - (CONV_W - 1):]


def setup_inputs(seed: int = 0) -> dict:
    key = jax.random.key(seed)
    ks = iter(jax.random.split(key, 48))

    def nrm(shape, scale):
        return jax.random.normal(next(ks), shape, F32) * scale

    n_pages = PAST_LEN // PAGE_SIZE
    n_pool = (DEC_BATCH * n_pages * 5 + 3) // 4
    page_table = jax.random.permutation(next(ks), n_pool)[: DEC_BATCH * n_pages].reshape(DEC_BATCH, n_pages).astype(jnp.int32)
    ones, beta = jnp.ones((MIX_W,), F32), jnp.full((MIX_W,), DN_BETA, F32)
    ev_col = jnp.concatenate([ones, ones, beta, ones, ones, beta, ones])
    od_col = jnp.concatenate([ones, ones, beta, ones, ones, ones, beta, ones, jnp.ones((2 * H_D,), F32)])
    kv_col = jnp.concatenate([jnp.ones((D_MODEL,), F32), jnp.full((D_MODEL,), DN_BETA, F32)])
    f_bias = jnp.linspace(3.0, 6.0, H_D, dtype=F32)[None, :] + nrm((N_ODD, H_D), 0.1)
    return {
        "x_prompt": nrm((BATCH, SEQ, D_MODEL), 1.0),
        "x_sample": nrm((DEC_BATCH, DEC_SEQ, D_MODEL), 1.0),
        "cache_a_k": nrm((N_EVEN, n_pool, PAGE_SIZE, H_A, 2, DH_A), 1.0),
        "cache_a_v": nrm((N_EVEN, n_pool, PAGE_SIZE, H_A, 2 * DH_A), 0.5),
        "state_b": nrm((N_EVEN, DEC_BATCH, H_B, DK_B, DV_B), 0.5),
        "state_c": nrm((N_ODD, DEC_BATCH, H_C, DK_C, DV_C), 0.5),
        "state_d_c": nrm((N_ODD, DEC_BATCH, H_D, DK_D, DV_D), 0.1),
        "state_d_n": nrm((N_ODD, DEC_BATCH, H_D, DK_D), 0.1),
        "state_d_m": nrm((N_ODD, DEC_BATCH, H_D), 1.0),
        "cache_mem_k": nrm((DEPTH, DEC_BATCH, N_MEM, H_X, DH_X), 1.0),
        "cache_mem_v": nrm((DEPTH, DEC_BATCH, N_MEM, H_X, DH_X), 0.5),
        "state_conv": nrm((DEPTH, DEC_BATCH, CONV_W - 1, D_FF), 1.0),
        "page_table": page_table,
        "mem_prompt": nrm((BATCH, N_MEM, D_MODEL), 1.0),
        "ev_w_in": nrm((N_EVEN, D_MODEL, EV_IN), D_MODEL ** -0.5) * ev_col,
        "ev_w_out": nrm((N_EVEN, 2 * MIX_W, D_MODEL), (2 * MIX_W) ** -0.5 * DN_BETA),
        "ev_lam": nrm((N_EVEN, 4, DH_A), 0.1),
        "ev_subln_g": 1.0 + nrm((N_EVEN, 2 * DH_A), 0.02),
        "ev_lb_logits": nrm((N_EVEN + 1, MIX_W), 0.1),
        "ev_b_norm_g": 1.0 + nrm((N_EVEN, DV_B), 0.02),
        "od_w_in": nrm((N_ODD, D_MODEL, OD_IN), D_MODEL ** -0.5) * od_col,
        "od_b_if": jnp.concatenate([nrm((N_ODD, H_D), 0.1), f_bias], -1),
        "od_w_out": nrm((N_ODD, 2 * MIX_W, D_MODEL), (2 * MIX_W) ** -0.5 * DN_BETA),
        "od_c_norm_g": 1.0 + nrm((N_ODD, DV_C), 0.02),
        "od_d_norm_g": 1.0 + nrm((N_ODD, DV_D), 0.02),
        "ln_g": 1.0 + nrm((DEPTH, 3, D_MODEL), 0.02),
        "ln_b": nrm((DEPTH, 3, D_MODEL), 0.02),
        "xa_wq": nrm((DEPTH, D_MODEL, D_MODEL), D_MODEL ** -0.5),
        "xa_wkv": nrm((DEPTH, D_MODEL, 2 * D_MODEL), D_MODEL ** -0.5) * kv_col,
        "xa_wo": nrm((DEPTH, D_MODEL, D_MODEL), D_MODEL ** -0.5 * DN_BETA),
        "ffn_w_up": nrm((DEPTH, D_MODEL, 2 * D_FF), D_MODEL ** -0.5),
        "ffn_conv_w": nrm((DEPTH, CONV_W, D_FF), CONV_W ** -0.5),
        "ffn_conv_b": nrm((DEPTH, D_FF), 0.02),
        "ffn_w_down": nrm((DEPTH, D_FF, D_MODEL), D_FF ** -0.5 * DN_BETA),
    }


def reference(x_prompt, x_sample, cache_a_k, cache_a_v, state_b, state_c, state_d_c, state_d_n, state_d_m,
              cache_mem_k, cache_mem_v, state_conv, page_table, mem_prompt,
              ev_w_in, ev_w_out, ev_lam, ev_subln_g, ev_lb_logits, ev_b_norm_g,
              od_w_in, od_b_if, od_w_out, od_c_norm_g, od_d_norm_g,
              ln_g, ln_b, xa_wq, xa_wkv, xa_wo, ffn_w_up, ffn_conv_w, ffn_conv_b, ffn_w_down):
    B, T_p = x_prompt.shape[0], x_prompt.shape[1]
    T_s = x_sample.shape[1]
    past = page_table.shape[1] * cache_a_k.shape[2]
    pos_p = jnp.arange(T_p)
    pos_s = past + jnp.arange(T_s)
    lb_table = jnp.cumsum(jax.nn.softmax(ev_lb_logits.astype(F32), axis=0), axis=0)

    xp, xs = x_prompt, x_sample
    ak_p, av_p, ak_s, av_s, sb_p, sb_s = [], [], [], [], [], []
    sc_p, sc_s, dc_p, dc_s, dn_p, dn_s, dm_p, dm_s = [], [], [], [], [], [], [], []
    mk_p, mv_p, cv_p, cv_s = [], [], [], []
    for l in range(DEPTH):
        j = l // 2
        if l % 2 == 0:
            lam_init = 0.8 - 0.6 * math.exp(-0.3 * l)
            yp, kp, vp, sbp = _even_mixer(xp, pos_p, jnp.zeros((B, H_B, DK_B, DV_B), F32), ev_w_in[j], ev_w_out[j],
                                          ev_lam[j], lam_init, ev_subln_g[j], lb_table[j], ev_b_norm_g[j],
                                          _attend_prompt)
            attend_s = functools.partial(_attend_sample, k_pool=cache_a_k[j], v_pool=cache_a_v[j], page_table=page_table)
            ys, ks_, vs_, sbs = _even_mixer(xs, pos_s, state_b[j], ev_w_in[j], ev_w_out[j],
                                            ev_lam[j], lam_init, ev_subln_g[j], lb_table[j], ev_b_norm_g[j],
                                            attend_s)
            ak_p.append(kp); av_p.append(vp); ak_s.append(ks_); av_s.append(vs_)
            sb_p.append(sbp); sb_s.append(sbs)
        else:
            yp, scp, dcp, dnp, dmp = _odd_mixer(xp, pos_p, jnp.zeros((B, H_C, DK_C, DV_C), F32),
                                                jnp.zeros((B, H_D, DK_D, DV_D), F32), jnp.zeros((B, H_D, DK_D), F32),
                                                jnp.zeros((B, H_D), F32), od_w_in[j], od_b_if[j], od_w_out[j],
                                                od_c_norm_g[j], od_d_norm_g[j])
            ys, scs, dcs, dns, dms = _odd_mixer(xs, pos_s, state_c[j], state_d_c[j], state_d_n[j], state_d_m[j],
                                                od_w_in[j], od_b_if[j], od_w_out[j], od_c_norm_g[j], od_d_norm_g[j])
            sc_p.append(scp); sc_s.append(scs); dc_p.append(dcp); dc_s.append(dcs)
            dn_p.append(dnp); dn_s.append(dns); dm_p.append(dmp); dm_s.append(dms)
        xp = _layernorm(DN_ALPHA * xp + yp, ln_g[l, 0], ln_b[l, 0])
        xs = _layernorm(DN_ALPHA * xs + ys, ln_g[l, 0], ln_b[l, 0])
        mkp, mvp = _mem_kv(mem_prompt, xa_wkv[l])
        mk_p.append(mkp); mv_p.append(mvp)
        xp = _layernorm(DN_ALPHA * xp + _cross_attn(xp, mkp, mvp, xa_wq[l], xa_wo[l]), ln_g[l, 1], ln_b[l, 1])
        xs = _layernorm(DN_ALPHA * xs + _cross_attn(xs, cache_mem_k[l], cache_mem_v[l], xa_wq[l], xa_wo[l]),
                        ln_g[l, 1], ln_b[l, 1])
        fp, cbp = _conv_ffn(xp, jnp.zeros((B, CONV_W - 1, D_FF), xp.dtype), ffn_w_up[l], ffn_conv_w[l], ffn_conv_b[l], ffn_w_down[l])
        fs, cbs = _conv_ffn(xs, state_conv[l], ffn_w_up[l], ffn_conv_w[l], ffn_conv_b[l], ffn_w_down[l])
        cv_p.append(cbp); cv_s.append(cbs)
        xp = _layernorm(DN_ALPHA * xp + fp, ln_g[l, 2], ln_b[l, 2])
        xs = _layernorm(DN_ALPHA * xs + fs, ln_g[l, 2], ln_b[l, 2])

    return (xp, xs,
            jnp.stack(ak_p), jnp.stack(av_p), jnp.stack(ak_s), jnp.stack(av_s),
            jnp.stack(sb_p), jnp.stack(sb_s),
            jnp.stack(sc_p), jnp.stack(sc_s),
            jnp.stack(dc_p), jnp.stack(dc_s), jnp.stack(dn_p), jnp.stack(dn_s), jnp.stack(dm_p), jnp.stack(dm_s),
            jnp.stack(mk_p), jnp.stack(mv_p),
            jnp.stack(cv_p), jnp.stack(cv_s))
```

```python
import numpy as np
from contextlib import ExitStack
import concourse.bass as bass
import concourse.mybir as mybir
from concourse.bass_utils import run_bass_kernel_spmd

F32 = mybir.dt.float32
BF16 = mybir.dt.bfloat16
I32 = mybir.dt.int32
U32 = mybir.dt.uint32
AF = mybir.ActivationFunctionType
ALU = mybir.AluOpType
AX = mybir.AxisListType

NDSEM = 6


class Buf:
    def __init__(self, name, h):
        self.name = name
        self.h = h
        self.last_w = None
        self.readers = {}

    def __getitem__(self, k):
        return self.h[k]


class Sched:
    ENG = ("pe", "act", "dve", "pool", "sp")

    def __init__(self, nc):
        self.nc = nc
        self.gstack = ExitStack()
        self.stack = self.gstack
        self.streams = {e: [] for e in self.ENG}
        self.cnt = {e: 0 for e in self.ENG}
        self.waited = {e: {} for e in self.ENG}
        self.dcnt = {e: 0 for e in self.ENG}
        self.sems = {}
        self.nops = 0
        for e in self.ENG:
            self.sem(e)
            for s_ in range(NDSEM):
                self.sem(("d", e, s_))

    def phase(self):
        outer = self

        class _P:
            def __enter__(self_):
                outer.stack = ExitStack()
                outer.barrier()
                return outer

            def __exit__(self_, *a):
                if a[0] is None:
                    outer.flush()
                outer.stack.close()
                outer.stack = outer.gstack
                return False

        return _P()

    def _dma_last(self, q, s):
        n = self.dcnt[q]
        if n <= s:
            return None
        last = ((n - 1 - s) // NDSEM) * NDSEM + s
        return (("d", q, s), 16 * (last // NDSEM + 1))

    def barrier(self, engines=None):
        for e in (engines or self.ENG):
            for f in self.ENG:
                if f != e and self.cnt[f]:
                    self._wait(e, (f, self.cnt[f]))
                for s_ in range(NDSEM):
                    self._wait(e, self._dma_last(f, s_))

    uid = 0

    def sb(self, name, shape, dt):
        Sched.uid += 1
        name = "%s_%d" % (name, Sched.uid)
        return Buf(name, self.stack.enter_context(self.nc.sbuf_tensor(name, list(shape), dt)))

    def ps(self, name, shape, dt):
        Sched.uid += 1
        name = "%s_%d" % (name, Sched.uid)
        return Buf(name, self.stack.enter_context(self.nc.psum_tensor(name, list(shape), dt)))

    def dram(self, name, shape, dt):
        return Buf(name, self.nc.dram_tensor(name, list(shape), dt, kind="Internal").ap())

    def ext(self, name, ap):
        return Buf(name, ap)

    def sem(self, key):
        if key not in self.sems:
            nm = "s_" + "_".join(str(k) for k in (key if isinstance(key, tuple) else (key,)))
            self.sems[key] = self.gstack.enter_context(self.nc.semaphore(nm))
        return self.sems[key]

    def _wait(self, eng, ev):
        if ev is None:
            return
        k, v = ev
        if k == eng and eng == "pe":
            return
        if self.waited[eng].get(k, 0) >= v:
            return
        self.waited[eng][k] = v
        self.streams[eng].append(("wait", k, v))

    def _deps(self, eng, reads, writes):
        for b in reads:
            self._wait(eng, b.last_w)
        for b in writes:
            self._wait(eng, b.last_w)
            for k, v in b.readers.items():
                self._wait(eng, (k, v))

    def _commit(self, ev, reads, writes):
        k, v = ev
        for b in reads:
            if b.readers.get(k, 0) < v:
                b.readers[k] = v
        for b in writes:
            b.last_w = ev
            b.readers = {}

    limit = 10 ** 9

    def op(self, eng, fn, reads=(), writes=()):
        if self.nops >= self.limit:
            return None
        self._deps(eng, reads, writes)
        self.cnt[eng] += 1
        ev = (eng, self.cnt[eng])
        self.streams[eng].append(("op", fn))
        self._commit(ev, reads, writes)
        self.nops += 1
        return ev

    def dma(self, q, fn, reads=(), writes=()):
        if self.nops >= self.limit:
            return None
        n = self.dcnt[q]
        self.dcnt[q] += 1
        slot = ("d", q, n % NDSEM)
        val = 16 * (n // NDSEM + 1)
        if n >= NDSEM:
            self._wait(q, (slot, val - 16))
        self._deps(q, reads, writes)
        self.streams[q].append(("dma", fn, slot))
        ev = (slot, val)
        self._commit(ev, reads, writes)
        self.nops += 1
        return ev

    def flush(self):
        nc = self.nc
        engmap = {"pe": "tensor", "act": "scalar", "dve": "vector", "pool": "gpsimd", "sp": "sync"}
        if not any(self.streams.values()):
            return
        with nc.Block() as block:
            for e in self.ENG:
                stream = self.streams[e]
                own = self.sem(e)

                def body(eng, stream=stream, own=own):
                    for rec in stream:
                        if rec[0] == "wait":
                            eng.wait_ge(self.sem(rec[1]), rec[2])
                        elif rec[0] == "op":
                            rec[1](eng).then_inc(own, 1)
                        else:
                            rec[1](eng).then_inc(self.sem(rec[2]), 16)

                getattr(block, engmap[e])(body)
        self.streams = {e: [] for e in self.ENG}

    def finish(self):
        self.barrier(engines=("sp",))
        self.flush()
        self.gstack.close()

    def make_identity(self, ident, dt_is_bf16=True):
        n = ident.h.shape[0]
        self.op("pool", lambda e: e.memset(ident[:], 1.0), writes=[ident])
        self.op("pool", lambda e: e.affine_select(out=ident[:], in_=ident[:], pattern=[[1, n]],
                                                   compare_op=ALU.is_equal, fill=0.0, base=0,
                                                   channel_multiplier=-1), reads=[ident], writes=[ident])


D = 1024
MIX = 512
DFF = 2816
NMEM = 256
EPS = 1e-5
DEPTH = 2
ALPHA = (2.0 * DEPTH) ** 0.25
NSEQ = 16
NPG = 16
GAMMA = [1.0 - 2.0 ** (-5.0 - h) for h in range(4)]


def build_program(NT, stop_after=None, NPOOL=2560):
    nc = bass.Bass("TRN2", target_bir_lowering=False)
    S = Sched(nc)
    NTL = NT // 128

    def din(name, shape, dt=F32):
        return nc.dram_tensor(name, list(shape), dt, kind="ExternalInput").ap()

    def dout(name, shape):
        return nc.dram_tensor(name, list(shape), F32, kind="ExternalOutput").ap()

    xp = din("xp", [NT, D])
    mem = din("mem", [NMEM, D])
    ropeA = din("ropeA", [NT, 128])
    ropeC = din("ropeC", [NT, 256])
    retg = din("retg", [128, 12])
    retdec = din("retdec", [128, 16])
    ev_w_in = din("ev_w_in", [D, 3584])
    ev_w_out = din("ev_w_out", [D, D])
    ev_lam = din("ev_lam", [1, 256])
    ev_subln_g = din("ev_subln_g", [1, 128])
    ev_lb = din("ev_lb", [1, 1024])
    ev_bng = din("ev_bng", [1, 128])
    od_w_in = din("od_w_in", [D, 4104])
    od_b_if = din("od_b_if", [1, 8])
    od_w_out = din("od_w_out", [D, D])
    od_cng = din("od_cng", [1, 128])
    od_dng = din("od_dng", [1, 128])
    ln_g = din("ln_g", [6, D])
    ln_b = din("ln_b", [6, D])
    xa_wq = din("xa_wq", [2, D, D])
    xa_wkv = din("xa_wkv", [2, D, 2 * D])
    xa_wo = din("xa_wo", [2, D, D])
    ffn_w_up = din("ffn_w_up", [2, D, 2 * DFF])
    ffn_cw = din("ffn_cw", [2, 3, DFF])
    ffn_cb = din("ffn_cb", [2, DFF])
    ffn_w_down = din("ffn_w_down", [2, DFF, D])

    y_p = dout("y_p", [NT, D])
    ak_p = dout("ak_p", [NT, 512])
    av_p = dout("av_p", [NT, 512])
    sb_p = dout("sb_p", [4, 128, 128])
    sc_p = dout("sc_p", [4, 128, 128])
    dc_p = dout("dc_p", [4, 128, 128])
    dn_p = dout("dn_p", [4, 128])
    dm_p = dout("dm_p", [4, 1])
    mk_p = dout("mk_p", [2, NMEM, D])
    mv_p = dout("mv_p", [2, NMEM, D])
    cv_p = dout("cv_p", [2, 2, DFF])

    X_d = S.dram("X_d", [NT, D], F32)
    X2_d = S.dram("X2_d", [NT, D], F32)
    OMIX_d = S.dram("OMIX_d", [NT, D], BF16)
    QT_d = S.dram("QT_d", [4, 128, NT], BF16)
    KT_d = S.dram("KT_d", [4, 128, NT], BF16)
    V_d = S.dram("V_d", [NT, 512], BF16)
    xp_b = S.ext("xp", xp)
    yp_b = S.ext("y_p", y_p)

    identb = S.sb("identb", [128, 128], BF16)
    identf = S.sb("identf", [128, 128], F32)
    onesf = S.sb("onesf", [128, 128], F32)
    zerosf = S.sb("zerosf", [128, 128], F32)
    TriT = S.sb("TriT", [128, 128], F32)
    AfterT = S.sb("AfterT", [128, 128], F32)
    mask2 = S.sb("mask2", [128, 64], F32)
    tribias = S.sb("tribias", [128, 128], F32)
    SelC = S.sb("SelC", [128, 2, 128], F32)
    S.make_identity(identb)
    S.make_identity(identf)
    S.op("pool", lambda e: e.memset(onesf[:], 1.0), writes=[onesf])
    S.op("pool", lambda e: e.memset(zerosf[:], 0.0), writes=[zerosf])
    S.op("pool", lambda e: e.memset(TriT[:], 0.0), writes=[TriT])
    S.op("pool", lambda e: e.memset(AfterT[:], 0.0), writes=[AfterT])
    for c in range(2):
        r = slice(c * 64, (c + 1) * 64)
        S.op("pool", lambda e, r=r: e.affine_select(out=TriT[r, r], in_=onesf[r, 0:64], pattern=[[1, 64]],
                                                     compare_op=ALU.is_ge, fill=0.0, base=0, channel_multiplier=-1),
             reads=[onesf], writes=[TriT])
        S.op("pool", lambda e, r=r: e.affine_select(out=AfterT[r, r], in_=onesf[r, 0:64], pattern=[[-1, 64]],
                                                     compare_op=ALU.is_ge, fill=0.0, base=-1, channel_multiplier=1),
             reads=[onesf], writes=[AfterT])
        S.op("pool", lambda e, r=r: e.affine_select(out=mask2[r, :], in_=onesf[r, 0:64], pattern=[[1, 64]],
                                                     compare_op=ALU.is_ge, fill=0.0, base=0, channel_multiplier=-1),
             reads=[onesf], writes=[mask2])
        S.op("pool", lambda e, c=c: e.affine_select(out=SelC[:, c, :], in_=onesf[:], pattern=[[0, 128]],
                                                     compare_op=ALU.is_equal, fill=0.0, base=-(c * 64 + 63),
                                                     channel_multiplier=1), reads=[onesf], writes=[SelC])
    S.op("pool", lambda e: e.affine_select(out=tribias[:], in_=zerosf[:], pattern=[[-1, 128]],
                                           compare_op=ALU.is_ge, fill=-30000.0, base=0, channel_multiplier=1),
         reads=[zerosf], writes=[tribias])

    def bcast_load(dst, src_row_ap, n):
        S.dma("sp", lambda e: e.dma_start(out=dst[:], in_=src_row_ap.partition_broadcast(128)), writes=[dst])

    def load_w_bf16(dst, src, K, N, col0=0, dcol0=0):
        kc = K // 128
        for k in range(kc):
            for c0 in range(0, N, 1024):
                w = min(1024, N - c0)
                S.dma("pool", lambda e, k=k, c0=c0, w=w: e.dma_start(
                    out=dst[:, k, dcol0 + c0:dcol0 + c0 + w], in_=src[k * 128:(k + 1) * 128, col0 + c0:col0 + c0 + w]),
                    writes=[dst])

    def transposes(src, nblk, P, tp, dst, eng="dve", ident=None, blk=128):
        ident = ident or identb
        for j in range(nblk):
            S.op("pe", lambda e, j=j: e.transpose(out=tp[0:blk, j, 0:P], in_=src[0:P, j * blk:(j + 1) * blk],
                                                  identity=ident[0:P, 0:P]), reads=[src, ident], writes=[tp])
        if eng == "act":
            S.op("act", lambda e: e.activation(out=dst[0:blk, 0:nblk, 0:P], in_=tp[0:blk, 0:nblk, 0:P], func=AF.Copy),
                 reads=[tp], writes=[dst])
        else:
            S.op(eng, lambda e: e.tensor_copy(out=dst[0:blk, 0:nblk, 0:P], in_=tp[0:blk, 0:nblk, 0:P]),
                 reads=[tp], writes=[dst])

    def proj_tm(ps, xT, W, kc, col0, n, P):
        for k in range(kc):
            S.op("pe", lambda e, k=k: e.matmul(ps[0:P, 0:n], lhsT=xT[:, k, 0:P], rhs=W[:, k, col0:col0 + n],
                                               start=(k == 0), stop=(k == kc - 1)), reads=[xT, W], writes=[ps])

    def layernorm(z, out, P, g_bc, b_bc, st, mv, rs):
        for hf in range(2):
            S.op("dve", lambda e, hf=hf: e.bn_stats(out=st[0:P, hf, :], in_=z[0:P, hf * 512:(hf + 1) * 512]),
                 reads=[z], writes=[st])
        S.op("dve", lambda e: e.bn_aggr(out=mv[0:P, :], in_=st[0:P, :, :]), reads=[st], writes=[mv])
        S.op("act", lambda e: e.activation(out=rs[0:P, :], in_=mv[0:P, 1:2], func=AF.Sqrt, bias=EPS, scale=1.0),
             reads=[mv], writes=[rs])
        S.op("dve", lambda e: e.reciprocal(out=rs[0:P, :], in_=rs[0:P, :]), reads=[rs], writes=[rs])
        S.op("dve", lambda e: e.tensor_scalar(out=out[0:P, :], in0=z[0:P, :], scalar1=mv[0:P, 0:1], scalar2=rs[0:P, 0:1],
                                              op0=ALU.subtract, op1=ALU.mult), reads=[z, mv, rs], writes=[out])
        S.op("pool", lambda e: e.tensor_tensor(out=out[0:P, :], in0=out[0:P, :], in1=g_bc[0:P, :], op=ALU.mult),
             reads=[out, g_bc], writes=[out])
        S.op("pool", lambda e: e.tensor_tensor(out=out[0:P, :], in0=out[0:P, :], in1=b_bc[0:P, :], op=ALU.add),
             reads=[out, b_bc], writes=[out])

    def gla_tile(qT, kT, kend, v, E, dec, St, Stb, pat, po, pds, atm):
        for c in range(2):
            r = slice(c * 64, (c + 1) * 64)
            for h in range(4):
                S.op("pe", lambda e, h=h, r=r: e.matmul(pat[r, h, :], lhsT=kT[:, h, r], rhs=qT[:, h, r],
                                                        start=True, stop=True), reads=[kT, qT], writes=[pat])
            S.op("dve", lambda e, r=r: e.tensor_tensor(out=atm[r, :, :], in0=pat[r, :, :],
                                                       in1=mask2[r, :].unsqueeze(1).to_broadcast([64, 4, 64]),
                                                       op=ALU.mult), reads=[pat, mask2], writes=[atm])
            for h in range(4):
                S.op("pe", lambda e, h=h, r=r: e.matmul(po[r, h, 0:E], lhsT=atm[r, h, :], rhs=v[r, h, 0:E],
                                                        start=True, stop=False), reads=[atm, v], writes=[po])
                S.op("pe", lambda e, h=h, r=r: e.matmul(po[r, h, 0:E], lhsT=qT[:, h, r], rhs=Stb[:, h, 0:E],
                                                        start=False, stop=True), reads=[qT, Stb], writes=[po])
            for h in range(4):
                S.op("pe", lambda e, h=h, r=r: e.matmul(pds[:, h, 0:E], lhsT=kend[r, h * 128:(h + 1) * 128],
                                                        rhs=v[r, h, 0:E], start=True, stop=True),
                     reads=[kend, v], writes=[pds])
            for h in range(4):
                S.op("dve", lambda e, h=h, c=c: e.scalar_tensor_tensor(
                    out=St[:, h, 0:E], in0=St[:, h, 0:E], scalar=dec[:, c * 4 + h, 0:1], in1=pds[:, h, 0:E],
                    op0=ALU.mult, op1=ALU.add), reads=[St, dec, pds], writes=[St])
            S.op("act", lambda e: e.activation(out=Stb[:, :, 0:E], in_=St[:, :, 0:E], func=AF.Copy),
                 reads=[St], writes=[Stb])

    def head_norm(srcb, src, dst32, P, sq, ss, rs, center):
        if center:
            S.op("dve", lambda e, src=src: e.tensor_reduce(out=ss[0:P, :], in_=src, axis=AX.X, op=ALU.add), reads=[srcb], writes=[ss])
            S.op("dve", lambda e: e.tensor_scalar(out=ss[0:P, :], in0=ss[0:P, :], scalar1=-1.0 / 128, scalar2=None,
                                                  op0=ALU.mult), reads=[ss], writes=[ss])
            S.op("dve", lambda e, src=src: e.tensor_tensor(out=dst32[0:P, :, :], in0=src,
                                                           in1=ss[0:P, :].unsqueeze(2).to_broadcast([P, 4, 128]), op=ALU.add),
                 reads=[ss, srcb], writes=[dst32])
            src2, srcb2 = dst32[0:P, :, :], dst32
        else:
            src2, srcb2 = src, srcb
        S.op("act", lambda e: e.activation(out=sq[0:P, :, :], in_=src2, func=AF.Square), reads=[srcb2], writes=[sq])
        S.op("dve", lambda e: e.tensor_reduce(out=ss[0:P, :], in_=sq[0:P, :, :], axis=AX.X, op=ALU.add),
             reads=[sq], writes=[ss])
        S.op("act", lambda e: e.activation(out=rs[0:P, :], in_=ss[0:P, :], func=AF.Sqrt, bias=EPS, scale=1.0 / 128),
             reads=[ss], writes=[rs])
        S.op("dve", lambda e: e.reciprocal(out=rs[0:P, :], in_=rs[0:P, :]), reads=[rs], writes=[rs])
        S.op("dve", lambda e: e.tensor_tensor(out=dst32[0:P, :, :], in0=src2,
                                              in1=rs[0:P, :].unsqueeze(2).to_broadcast([P, 4, 128]), op=ALU.mult),
             reads=[rs, srcb2], writes=[dst32])

    lam_init0 = 0.8 - 0.6 * 1.0

    def phase_A0():
        with S.phase():
            W = S.sb("W0", [128, 8, 3584], BF16)
            load_w_bf16(W, ev_w_in, D, 3584)
            lbraw = S.sb("lbraw", [128, 2, 512], F32)
            lb_bc = S.sb("lb_bc", [128, 512], F32)
            oml_bc = S.sb("oml_bc", [128, 512], F32)
            bng_bc = S.sb("bng_bc", [128, 128], F32)
            bcast_load(lbraw, ev_lb, 1024)
            bcast_load(bng_bc, ev_bng, 128)
            S.op("act", lambda e: e.activation(out=lbraw[:], in_=lbraw[:], func=AF.Exp), reads=[lbraw], writes=[lbraw])
            S.op("dve", lambda e: e.tensor_tensor(out=oml_bc[:], in0=lbraw[:, 0, :], in1=lbraw[:, 1, :], op=ALU.add),
                 reads=[lbraw], writes=[oml_bc])
            S.op("dve", lambda e: e.reciprocal(out=oml_bc[:], in_=oml_bc[:]), reads=[oml_bc], writes=[oml_bc])
            S.op("dve", lambda e: e.tensor_tensor(out=lb_bc[:], in0=lbraw[:, 0, :], in1=oml_bc[:], op=ALU.mult),
                 reads=[lbraw, oml_bc], writes=[lb_bc])
            S.op("dve", lambda e: e.tensor_tensor(out=oml_bc[:], in0=lbraw[:, 1, :], in1=oml_bc[:], op=ALU.mult),
                 reads=[lbraw, oml_bc], writes=[oml_bc])

            xin = S.sb("xin", [128, D], F32)
            xb = S.sb("xb", [128, D], BF16)
            xT = S.sb("xT", [128, 8, 128], BF16)
            rope = S.sb("rope", [128, 128], F32)
            praw = S.sb("praw", [128, 512], F32)
            t1 = S.sb("t1", [128, 8, 32], F32)
            t2 = S.sb("t2", [128, 8, 32], F32)
            t3 = S.sb("t3", [128, 8, 32], F32)
            t4 = S.sb("t4", [128, 8, 32], F32)
            kr = S.sb("kr", [128, 512], F32)
            rb = S.sb("rb", [128, 512], BF16)
            rT = S.sb("rT", [128, 4, 128], BF16)
            v32 = S.sb("v32", [128, 512], F32)
            vb = S.sb("vb", [128, 512], BF16)
            sq = S.sb("sq", [128, 512], F32)
            ff = S.sb("ff", [128, 512], F32)
            logf = S.sb("logf", [128, 512], F32)
            k1 = S.sb("k1", [128, 512], F32)
            eb = S.sb("eb", [128, 512], F32)
            qin = S.sb("qin", [128, 512], BF16)
            kin = S.sb("kin", [128, 512], BF16)
            kend = S.sb("kend", [128, 512], BF16)
            vh = S.sb("vh", [128, 4, 128], BF16)
            gate = S.sb("gate", [128, 512], F32)
            dec = S.sb("dec", [128, 8, 2], F32)
            qinT = S.sb("qinT", [128, 4, 128], BF16)
            kinT = S.sb("kinT", [128, 4, 128], BF16)
            atm = S.sb("atm", [128, 4, 64], BF16)
            St = S.sb("St", [128, 4, 128], F32)
            Stb = S.sb("Stb", [128, 4, 128], BF16)
            on32 = S.sb("on32", [128, 4, 128], F32)
            sqq = S.sb("sqq", [128, 4, 128], F32)
            ss = S.sb("ss", [128, 4], F32)
            rs = S.sb("rs", [128, 4], F32)
            omb = S.sb("omb", [128, 512], BF16)
            tp = S.ps("tp", [128, 8, 128], BF16)
            pj = [S.ps("pj%d" % i, [128, 512], F32) for i in range(3)]
            pdec = S.ps("pdec", [128, 8, 2], F32)
            pat = S.ps("pat", [128, 4, 64], F32)
            po = S.ps("po", [128, 4, 128], F32)
            pds = S.ps("pds", [128, 4, 128], F32)
            S.op("pool", lambda e: e.memset(St[:], 0.0), writes=[St])
            S.op("pool", lambda e: e.memset(Stb[:], 0.0), writes=[Stb])
            pjn = [0]

            def nextpj():
                p = pj[pjn[0] % 3]
                pjn[0] += 1
                return p

            def do_rope(src, dst, c0, nm, f):
                sv = src[:].rearrange("p (m two f) -> p m two f", two=2, f=f)
                dv = dst[:].rearrange("p (m two f) -> p m two f", two=2, f=f)
                cb = rope[:, c0:c0 + f].unsqueeze(1).to_broadcast([128, nm, f])
                sb_ = rope[:, c0 + f:c0 + 2 * f].unsqueeze(1).to_broadcast([128, nm, f])
                S.op("dve", lambda e: e.tensor_tensor(out=t1[:], in0=sv[:, :, 0, :], in1=cb, op=ALU.mult), reads=[src, rope], writes=[t1])
                S.op("pool", lambda e: e.tensor_tensor(out=t2[:], in0=sv[:, :, 1, :], in1=sb_, op=ALU.mult), reads=[src, rope], writes=[t2])
                S.op("dve", lambda e: e.tensor_tensor(out=dv[:, :, 0, :], in0=t1[:], in1=t2[:], op=ALU.subtract), reads=[t1, t2], writes=[dst])
                S.op("pool", lambda e: e.tensor_tensor(out=t3[:], in0=sv[:, :, 1, :], in1=cb, op=ALU.mult), reads=[src, rope], writes=[t3])
                S.op("dve", lambda e: e.tensor_tensor(out=t4[:], in0=sv[:, :, 0, :], in1=sb_, op=ALU.mult), reads=[src, rope], writes=[t4])
                S.op("pool", lambda e: e.tensor_tensor(out=dv[:, :, 1, :], in0=t3[:], in1=t4[:], op=ALU.add), reads=[t3, t4], writes=[dst])

            for t in range(NTL):
                tr = slice(t * 128, (t + 1) * 128)
                S.dma("sp", lambda e, tr=tr: e.dma_start(out=xin[:], in_=xp[tr, :]), reads=[xp_b], writes=[xin])
                S.dma("sp", lambda e, tr=tr: e.dma_start(out=rope[:], in_=ropeA[tr, :]), writes=[rope])
                S.op("act", lambda e: e.activation(out=xb[:], in_=xin[:], func=AF.Copy), reads=[xin], writes=[xb])
                transposes(xb, 8, 128, tp, xT)
                p = nextpj(); proj_tm(p, xT, W, 8, 0, 512, 128)
                S.op("act", lambda e, p=p: e.activation(out=praw[:], in_=p[:], func=AF.Copy), reads=[p], writes=[praw])
                do_rope(praw, rb, 0, 8, 32)
                transposes(rb, 4, 128, tp, rT)
                S.dma("sp", lambda e, tr=tr: e.dma_start(out=QT_d[:].rearrange("h p n -> p h n")[:, :, tr], in_=rT[:]),
                      reads=[rT], writes=[QT_d])
                p = nextpj(); proj_tm(p, xT, W, 8, 512, 512, 128)
                S.op("act", lambda e, p=p: e.activation(out=praw[:], in_=p[:], func=AF.Copy), reads=[p], writes=[praw])
                do_rope(praw, kr, 64, 8, 32)
                S.dma("sp", lambda e, tr=tr: e.dma_start(out=ak_p[tr, :], in_=kr[:]), reads=[kr])
                S.op("act", lambda e: e.activation(out=rb[:], in_=kr[:], func=AF.Copy), reads=[kr], writes=[rb])
                transposes(rb, 4, 128, tp, rT)
                S.dma("sp", lambda e, tr=tr: e.dma_start(out=KT_d[:].rearrange("h p n -> p h n")[:, :, tr], in_=rT[:]),
                      reads=[rT], writes=[KT_d])
                p = nextpj(); proj_tm(p, xT, W, 8, 1024, 512, 128)
                S.op("act", lambda e, p=p: e.activation(out=v32[:], in_=p[:], func=AF.Copy), reads=[p], writes=[v32])
                S.op("dve", lambda e: e.tensor_copy(out=vb[:], in_=v32[:]), reads=[v32], writes=[vb])
                S.dma("sp", lambda e, tr=tr: e.dma_start(out=av_p[tr, :], in_=v32[:]), reads=[v32])
                S.dma("sp", lambda e, tr=tr: e.dma_start(out=V_d[tr, :], in_=vb[:]), reads=[vb], writes=[V_d])
                p = nextpj(); proj_tm(p, xT, W, 8, 1536, 512, 128)
                S.op("act", lambda e, p=p: e.activation(out=sq[:], in_=p[:], func=AF.Silu), reads=[p], writes=[sq])
                p = nextpj(); proj_tm(p, xT, W, 8, 2048, 512, 128)
                S.op("act", lambda e, p=p: e.activation(out=ff[:], in_=p[:], func=AF.Sigmoid), reads=[p], writes=[ff])
                S.op("dve", lambda e: e.tensor_tensor(out=ff[:], in0=ff[:], in1=oml_bc[:], op=ALU.mult), reads=[ff, oml_bc], writes=[ff])
                S.op("dve", lambda e: e.tensor_tensor(out=ff[:], in0=ff[:], in1=lb_bc[:], op=ALU.add), reads=[ff, lb_bc], writes=[ff])
                S.op("act", lambda e: e.activation(out=logf[:], in_=ff[:], func=AF.Ln), reads=[ff], writes=[logf])
                S.op("pool", lambda e: e.tensor_scalar(out=k1[:], in0=ff[:], scalar1=-1.0, scalar2=1.0, op0=ALU.mult, op1=ALU.add),
                     reads=[ff], writes=[k1])
                pb = nextpj(); pa = nextpj()
                S.op("pe", lambda e, pb=pb: e.matmul(pb[:], lhsT=TriT[:], rhs=logf[:], start=True, stop=True), reads=[TriT, logf], writes=[pb])
                S.op("pe", lambda e, pa=pa: e.matmul(pa[:], lhsT=AfterT[:], rhs=logf[:], start=True, stop=True), reads=[AfterT, logf], writes=[pa])
                for c in range(2):
                    for h in range(4):
                        S.op("pe", lambda e, c=c, h=h: e.matmul(pdec[:, c * 4 + h, :], lhsT=logf[c * 64:(c + 1) * 64, h * 128:(h + 1) * 128],
                                                                rhs=onesf[c * 64:(c + 1) * 64, 0:2], start=True, stop=True),
                             reads=[logf, onesf], writes=[pdec])
                S.op("act", lambda e: e.activation(out=dec[:], in_=pdec[:], func=AF.Exp), reads=[pdec], writes=[dec])
                S.op("act", lambda e, pb=pb: e.activation(out=eb[:], in_=pb[:], func=AF.Exp), reads=[pb], writes=[eb])
                S.op("dve", lambda e: e.tensor_tensor(out=qin[:], in0=sq[:], in1=eb[:], op=ALU.mult), reads=[sq, eb], writes=[qin])
                S.op("act", lambda e, pb=pb: e.activation(out=eb[:], in_=pb[:], func=AF.Exp, scale=-1.0), reads=[pb], writes=[eb])
                S.op("dve", lambda e: e.tensor_tensor(out=kin[:], in0=k1[:], in1=eb[:], op=ALU.mult), reads=[k1, eb], writes=[kin])
                S.op("act", lambda e, pa=pa: e.activation(out=eb[:], in_=pa[:], func=AF.Exp), reads=[pa], writes=[eb])
                S.op("dve", lambda e: e.tensor_tensor(out=kend[:], in0=k1[:], in1=eb[:], op=ALU.mult), reads=[k1, eb], writes=[kend])
                p = nextpj(); proj_tm(p, xT, W, 8, 2560, 512, 128)
                S.op("act", lambda e, p=p: e.activation(out=vh[:].rearrange("p h e -> p (h e)"), in_=p[:], func=AF.Copy), reads=[p], writes=[vh])
                p = nextpj(); proj_tm(p, xT, W, 8, 3072, 512, 128)
                S.op("act", lambda e, p=p: e.activation(out=gate[:], in_=p[:], func=AF.Silu), reads=[p], writes=[gate])
                transposes(qin, 4, 128, tp, qinT)
                transposes(kin, 4, 128, tp, kinT)
                gla_tile(qinT, kinT, kend, vh, 128, dec, St, Stb, pat, po, pds, atm)
                head_norm(po, po[:, :, :], on32, 128, sqq, ss, rs, center=False)
                S.op("pool", lambda e: e.tensor_tensor(out=on32[:], in0=on32[:], in1=bng_bc[:].unsqueeze(1).to_broadcast([128, 4, 128]), op=ALU.mult),
                     reads=[on32, bng_bc], writes=[on32])
                S.op("dve", lambda e: e.tensor_tensor(out=omb[:], in0=on32[:].rearrange("p h e -> p (h e)"), in1=gate[:], op=ALU.mult),
                     reads=[on32, gate], writes=[omb])
                S.dma("sp", lambda e, tr=tr: e.dma_start(out=OMIX_d[tr, 512:1024], in_=omb[:]), reads=[omb], writes=[OMIX_d])
            S.dma("sp", lambda e: e.dma_start(out=sb_p.rearrange("h d e -> d h e"), in_=St[:]), reads=[St])


    def phase_B0():
        with S.phase():
            KT = S.sb("KT", [128, NT], BF16)
            V = S.sb("V", [128, NTL, 128], BF16)
            lamraw = S.sb("lamraw", [128, 256], F32)
            lamp = S.sb("lamp", [128, 2, 64], F32)
            lams = S.sb("lams", [128, 2], F32)
            nlam = S.sb("nlam", [128, 1], F32)
            sg_bc = S.sb("sg_bc", [128, 128], F32)
            bcast_load(lamraw, ev_lam, 256)
            bcast_load(sg_bc, ev_subln_g, 128)
            lv = lamraw[:].rearrange("p (a b d) -> p a b d", a=2, b=2)
            S.op("dve", lambda e: e.tensor_tensor(out=lamp[:], in0=lv[:, :, 0, :], in1=lv[:, :, 1, :], op=ALU.mult), reads=[lamraw], writes=[lamp])
            S.op("dve", lambda e: e.tensor_reduce(out=lams[:], in_=lamp[:], axis=AX.X, op=ALU.add), reads=[lamp], writes=[lams])
            S.op("act", lambda e: e.activation(out=lams[:], in_=lams[:], func=AF.Exp), reads=[lams], writes=[lams])
            S.op("dve", lambda e: e.tensor_tensor(out=nlam[:], in0=lams[:, 1:2], in1=lams[:, 0:1], op=ALU.subtract), reads=[lams], writes=[nlam])
            S.op("dve", lambda e: e.tensor_scalar(out=nlam[:], in0=nlam[:], scalar1=-lam_init0, scalar2=None, op0=ALU.add), reads=[nlam], writes=[nlam])
            S.op("dve", lambda e: e.tensor_scalar(out=sg_bc[:], in0=sg_bc[:], scalar1=1.0 - lam_init0, scalar2=None, op0=ALU.mult), reads=[sg_bc], writes=[sg_bc])

            QT = S.sb("QT", [128, 128], BF16)
            Sf = S.sb("Sf", [128, NT], F32)
            Pb = S.sb("Pb", [128, NT], BF16)
            PT = S.sb("PT", [128, 4, 128], BF16)
            mx = S.sb("mx", [128, 1], F32)
            ll = S.sb("ll", [128, 2], F32)
            rl = S.sb("rl", [128, 2], F32)
            oa = S.sb("oa", [128, 128], F32)
            sqq = S.sb("sqqb", [128, 128], F32)
            ss = S.sb("ssb", [128, 1], F32)
            rs = S.sb("rsb", [128, 1], F32)
            omb = S.sb("ombb", [128, 128], BF16)
            psS = [S.ps("psS%d" % i, [128, 512], F32) for i in range(2)]
            ptp = [S.ps("ptp%d" % i, [128, 4, 128], BF16) for i in range(2)]
            pO = [S.ps("pO%d" % i, [128, 128], F32) for i in range(2)]
            cnt = [0]
            for h in range(4):
                S.dma("sp", lambda e, h=h: e.dma_start(out=KT[:], in_=KT_d[h]), reads=[KT_d], writes=[KT])
                S.dma("sp", lambda e, h=h: e.dma_start(out=V[:], in_=V_d[:, h * 128:(h + 1) * 128].rearrange("(t p) f -> p t f", p=128)), reads=[V_d], writes=[V])
                for i in range(NTL):
                    tr = slice(i * 128, (i + 1) * 128)
                    n = (i + 1) * 128
                    S.dma("sp", lambda e, tr=tr, h=h: e.dma_start(out=QT[:], in_=QT_d[h, :, tr]), reads=[QT_d], writes=[QT])
                    for c in range(2):
                        cr = slice(c * 64, (c + 1) * 64)
                        for k0 in range(0, n, 512):
                            w = min(512, n - k0)
                            ps_ = psS[cnt[0] % 2]; cnt[0] += 1
                            S.op("pe", lambda e, ps_=ps_, cr=cr, k0=k0, w=w: e.matmul(ps_[:, 0:w], lhsT=QT[cr, :], rhs=KT[cr, k0:k0 + w], start=True, stop=True),
                                 reads=[QT, KT], writes=[ps_])
                            if k0 + w == n:
                                if w > 128:
                                    S.op("dve", lambda e, ps_=ps_, k0=k0, w=w: e.tensor_copy(out=Sf[:, k0:k0 + w - 128], in_=ps_[:, 0:w - 128]), reads=[ps_], writes=[Sf])
                                S.op("dve", lambda e, ps_=ps_, w=w, n=n: e.tensor_tensor(out=Sf[:, n - 128:n], in0=ps_[:, w - 128:w], in1=tribias[:], op=ALU.add),
                                     reads=[ps_, tribias], writes=[Sf])
                            else:
                                S.op("dve", lambda e, ps_=ps_, k0=k0, w=w: e.tensor_copy(out=Sf[:, k0:k0 + w], in_=ps_[:, 0:w]), reads=[ps_], writes=[Sf])
                        S.op("dve", lambda e, n=n: e.tensor_reduce(out=mx[:], in_=Sf[:, 0:n], axis=AX.X, op=ALU.max, negate=True), reads=[Sf], writes=[mx])
                        S.op("act", lambda e, n=n, c=c: e.activation(out=Pb[:, 0:n], in_=Sf[:, 0:n], func=AF.Exp, bias=mx[:, 0:1], scale=1.0,
                                                                    accum_out=ll[:, c:c + 1]), reads=[Sf, mx], writes=[Pb, ll])
                        po_ = pO[c]
                        for j0 in range(0, i + 1, 4):
                            nj = min(4, i + 1 - j0)
                            tp_ = ptp[(j0 // 4) % 2]
                            for j in range(nj):
                                S.op("pe", lambda e, tp_=tp_, j=j, j0=j0: e.transpose(out=tp_[:, j, :], in_=Pb[:, (j0 + j) * 128:(j0 + j + 1) * 128], identity=identb[:]),
                                     reads=[Pb, identb], writes=[tp_])
                            S.op("act" if (j0 // 4) % 2 else "dve",
                                 (lambda e, tp_=tp_, nj=nj: e.activation(out=PT[:, 0:nj, :], in_=tp_[:, 0:nj, :], func=AF.Copy)) if (j0 // 4) % 2 else
                                 (lambda e, tp_=tp_, nj=nj: e.tensor_copy(out=PT[:, 0:nj, :], in_=tp_[:, 0:nj, :])), reads=[tp_], writes=[PT])
                            for j in range(nj):
                                S.op("pe", lambda e, po_=po_, j=j, j0=j0: e.matmul(po_[:], lhsT=PT[:, j, :], rhs=V[:, j0 + j, :],
                                                                                   start=(j0 + j == 0), stop=(j0 + j == i)), reads=[PT, V], writes=[po_])
                    S.op("dve", lambda e: e.reciprocal(out=rl[:], in_=ll[:]), reads=[ll], writes=[rl])
                    S.op("dve", lambda e: e.tensor_tensor(out=rl[:, 1:2], in0=rl[:, 1:2], in1=nlam[:], op=ALU.mult), reads=[rl, nlam], writes=[rl])
                    S.op("act", lambda e: e.activation(out=oa[:], in_=pO[0][:], func=AF.Copy, scale=rl[:, 0:1]), reads=[pO[0], rl], writes=[oa])
                    S.op("dve", lambda e: e.scalar_tensor_tensor(out=oa[:], in0=pO[1][:], scalar=rl[:, 1:2], in1=oa[:], op0=ALU.mult, op1=ALU.add),
                         reads=[pO[1], rl, oa], writes=[oa])
                    S.op("act", lambda e: e.activation(out=sqq[:], in_=oa[:], func=AF.Square, accum_out=ss[:]), reads=[oa], writes=[sqq, ss])
                    S.op("act", lambda e: e.activation(out=rs[:], in_=ss[:], func=AF.Sqrt, bias=EPS, scale=1.0 / 128), reads=[ss], writes=[rs])
                    S.op("dve", lambda e: e.reciprocal(out=rs[:], in_=rs[:]), reads=[rs], writes=[rs])
                    S.op("dve", lambda e: e.scalar_tensor_tensor(out=omb[:], in0=oa[:], scalar=rs[:, 0:1], in1=sg_bc[:], op0=ALU.mult, op1=ALU.mult),
                         reads=[oa, rs, sg_bc], writes=[omb])
                    S.dma("sp", lambda e, tr=tr, h=h: e.dma_start(out=OMIX_d[tr, h * 128:(h + 1) * 128], in_=omb[:]), reads=[omb], writes=[OMIX_d])

    def phase_C1(l, w_out_ap, xsrc_b, xsrc_ap):
        with S.phase():
            Wo = S.sb("Wo", [128, 8, D], BF16); load_w_bf16(Wo, w_out_ap, D, D)
            Wq = S.sb("Wq", [128, 8, D], BF16); load_w_bf16(Wq, xa_wq[l], D, D)
            Wx = S.sb("Wx", [128, 8, D], BF16); load_w_bf16(Wx, xa_wo[l], D, D)
            Wkv = S.sb("Wkv", [128, 8, 2 * D], BF16); load_w_bf16(Wkv, xa_wkv[l], D, 2 * D)
            g1 = S.sb("g1", [128, D], F32); b1 = S.sb("b1", [128, D], F32)
            g2 = S.sb("g2", [128, D], F32); b2 = S.sb("b2", [128, D], F32)
            bcast_load(g1, ln_g[3 * l:3 * l + 1, :], D); bcast_load(b1, ln_b[3 * l:3 * l + 1, :], D)
            bcast_load(g2, ln_g[3 * l + 1:3 * l + 2, :], D); bcast_load(b2, ln_b[3 * l + 1:3 * l + 2, :], D)
            xin = S.sb("c_xin", [128, D], F32)
            om = S.sb("c_om", [128, D], BF16)
            oT = S.sb("c_oT", [128, 8, 128], BF16)
            zz = S.sb("c_z", [128, D], F32)
            x1 = S.sb("c_x1", [128, D], F32)
            x1b = S.sb("c_x1b", [128, D], BF16)
            qTb = S.sb("c_qTb", [128, 8, 128], BF16)
            Pc = S.sb("c_Pc", [128, 1024], BF16)
            PT = S.sb("c_PT", [128, 8, 128], BF16)
            mkT = S.sb("c_mkT", [128, 8, 256], BF16)
            mvb = S.sb("c_mvb", [128, 2, D], BF16)
            m32 = S.sb("c_m32", [128, 512], F32)
            st = S.sb("c_st", [128, 2, 6], F32); mv_ = S.sb("c_mv", [128, 2], F32); rs = S.sb("c_rs", [128, 1], F32)
            nmx = S.sb("c_nmx", [128, 4], F32); ll = S.sb("c_ll", [128, 4], F32)
            tp = S.ps("c_tp", [128, 8, 128], BF16)
            pj = [S.ps("c_pj%d" % i, [128, 512], F32) for i in range(2)]
            pq = S.ps("c_pq", [128, 8, 128], F32)
            psc = S.ps("c_psc", [128, 4, 256], F32)
            for mt in range(2):
                S.dma("sp", lambda e, mt=mt: e.dma_start(out=xin[:], in_=mem[mt * 128:(mt + 1) * 128, :]), writes=[xin])
                S.op("act", lambda e: e.activation(out=om[:], in_=xin[:], func=AF.Copy), reads=[xin], writes=[om])
                transposes(om, 8, 128, tp, oT)
                for g in range(4):
                    p = pj[g % 2]
                    proj_tm(p, oT, Wkv, 8, g * 512, 512, 128)
                    S.op("act", lambda e, p=p: e.activation(out=m32[:], in_=p[:], func=AF.Copy), reads=[p], writes=[m32])
                    dst = mk_p if g < 2 else mv_p
                    S.dma("sp", lambda e, dst=dst, g=g, mt=mt: e.dma_start(out=dst[l, mt * 128:(mt + 1) * 128, (g % 2) * 512:(g % 2 + 1) * 512], in_=m32[:]), reads=[m32])
                    if g >= 2:
                        S.op("dve", lambda e, g=g, mt=mt: e.tensor_copy(out=mvb[:, mt, (g - 2) * 512:(g - 1) * 512], in_=m32[:]), reads=[m32], writes=[mvb])
                for ch in range(8):
                    for k in range(8):
                        S.op("pe", lambda e, ch=ch, k=k: e.matmul(pq[:, ch, :], lhsT=Wkv[:, k, ch * 128:(ch + 1) * 128], rhs=oT[:, k, :], start=(k == 0), stop=(k == 7)),
                             reads=[Wkv, oT], writes=[pq])
                S.op("act", lambda e, mt=mt: e.activation(out=mkT[:, :, mt * 128:(mt + 1) * 128], in_=pq[:], func=AF.Copy), reads=[pq], writes=[mkT])
            for t in range(NTL):
                tr = slice(t * 128, (t + 1) * 128)
                S.dma("sp", lambda e, tr=tr: e.dma_start(out=xin[:], in_=xsrc_ap[tr, :]), reads=[xsrc_b], writes=[xin])
                S.dma("sp", lambda e, tr=tr: e.dma_start(out=om[:], in_=OMIX_d[tr, :]), reads=[OMIX_d], writes=[om])
                transposes(om, 8, 128, tp, oT)
                for g in range(2):
                    proj_tm(pj[g], oT, Wo, 8, g * 512, 512, 128)
                    S.op("dve", lambda e, g=g: e.scalar_tensor_tensor(out=zz[:, g * 512:(g + 1) * 512], in0=xin[:, g * 512:(g + 1) * 512], scalar=ALPHA, in1=pj[g][:], op0=ALU.mult, op1=ALU.add),
                         reads=[xin, pj[g]], writes=[zz])
                layernorm(zz, x1, 128, g1, b1, st, mv_, rs)
                S.op("act", lambda e: e.activation(out=x1b[:], in_=x1[:], func=AF.Copy), reads=[x1], writes=[x1b])
                transposes(x1b, 8, 128, tp, oT)
                for ch in range(8):
                    for k in range(8):
                        S.op("pe", lambda e, ch=ch, k=k: e.matmul(pq[:, ch, :], lhsT=Wq[:, k, ch * 128:(ch + 1) * 128], rhs=oT[:, k, :], start=(k == 0), stop=(k == 7)),
                             reads=[Wq, oT], writes=[pq])
                S.op("act", lambda e: e.activation(out=qTb[:], in_=pq[:], func=AF.Copy, scale=1.0 / 16), reads=[pq], writes=[qTb])
                for h in range(4):
                    for dc in range(2):
                        S.op("pe", lambda e, h=h, dc=dc: e.matmul(psc[:, h, :], lhsT=qTb[:, 2 * h + dc, :], rhs=mkT[:, 2 * h + dc, :], start=(dc == 0), stop=(dc == 1)),
                             reads=[qTb, mkT], writes=[psc])
                S.op("dve", lambda e: e.tensor_reduce(out=nmx[:], in_=psc[:], axis=AX.X, op=ALU.max, negate=True), reads=[psc], writes=[nmx])
                for h in range(4):
                    S.op("act", lambda e, h=h: e.activation(out=Pc[:, h * 256:(h + 1) * 256], in_=psc[:, h, :], func=AF.Exp, bias=nmx[:, h:h + 1], scale=1.0, accum_out=ll[:, h:h + 1]),
                         reads=[psc, nmx], writes=[Pc, ll])
                S.op("dve", lambda e: e.reciprocal(out=ll[:], in_=ll[:]), reads=[ll], writes=[ll])
                transposes(Pc, 8, 128, tp, PT)
                po = pq
                pov = po[:].rearrange("p a b -> p (a b)")
                for h in range(4):
                    for mc in range(2):
                        S.op("pe", lambda e, h=h, mc=mc: e.matmul(pov[:, h * 256:(h + 1) * 256], lhsT=PT[:, 2 * h + mc, :], rhs=mvb[:, mc, h * 256:(h + 1) * 256], start=(mc == 0), stop=(mc == 1)),
                             reads=[PT, mvb], writes=[po])
                for h in range(4):
                    S.op("act", lambda e, h=h: e.activation(out=om[:, h * 256:(h + 1) * 256], in_=pov[:, h * 256:(h + 1) * 256], func=AF.Copy, scale=ll[:, h:h + 1]),
                         reads=[po, ll], writes=[om])
                transposes(om, 8, 128, tp, oT)
                for g in range(2):
                    proj_tm(pj[g], oT, Wx, 8, g * 512, 512, 128)
                    S.op("dve", lambda e, g=g: e.scalar_tensor_tensor(out=zz[:, g * 512:(g + 1) * 512], in0=x1[:, g * 512:(g + 1) * 512], scalar=ALPHA, in1=pj[g][:], op0=ALU.mult, op1=ALU.add),
                         reads=[x1, pj[g]], writes=[zz])
                layernorm(zz, xin, 128, g2, b2, st, mv_, rs)
                S.dma("sp", lambda e, tr=tr: e.dma_start(out=X2_d[tr, :], in_=xin[:]), reads=[xin], writes=[X2_d])

    def phase_C2(l, xdst_b, xdst_ap):
        with S.phase():
            Wu = S.sb("Wu", [128, 8, 2 * DFF], BF16); load_w_bf16(Wu, ffn_w_up[l], D, 2 * DFF)
            Wd = S.sb("Wd", [128, 22, D], BF16); load_w_bf16(Wd, ffn_w_down[l], DFF, D)
            g3 = S.sb("g3", [128, D], F32); b3 = S.sb("b3", [128, D], F32)
            bcast_load(g3, ln_g[3 * l + 2:3 * l + 3, :], D); bcast_load(b3, ln_b[3 * l + 2:3 * l + 3, :], D)
            cw = S.sb("cw", [128, 3, 22], F32); cb = S.sb("cbb", [128, 22], F32)
            with nc.allow_non_contiguous_dma(reason="tiny conv params"):
                pass
            S.dma("sp", lambda e: e.dma_start(out=cw[:], in_=ffn_cw[l].rearrange("i (j p) -> p i j", p=128), allow_slow_non_contiguous=True), writes=[cw])
            S.dma("sp", lambda e: e.dma_start(out=cb[:], in_=ffn_cb[l].rearrange("(j p) -> p j", p=128), allow_slow_non_contiguous=True), writes=[cb])
            x2 = S.sb("f_x2", [128, D], F32)
            x2b = S.sb("f_x2b", [128, D], BF16)
            x2T = S.sb("f_x2T", [128, 8, 128], BF16)
            upb = S.sb("f_upb", [128, 22, 130], F32)
            cv = S.sb("f_cv", [128, 128], F32)
            ge = S.sb("f_ge", [128, 128], F32)
            hT = S.sb("f_hT", [128, 22, 128], BF16)
            zz = S.sb("f_z", [128, D], F32)
            st = S.sb("f_st", [128, 2, 6], F32); mv_ = S.sb("f_mv", [128, 2], F32); rs = S.sb("f_rs", [128, 1], F32)
            cvo = S.sb("f_cvo", [2, DFF], F32)
            tp = S.ps("f_tp", [128, 8, 128], BF16)
            pu = [S.ps("f_pu%d" % i, [128, 128], F32) for i in range(2)]
            pg = [S.ps("f_pg%d" % i, [128, 128], F32) for i in range(2)]
            pj = [S.ps("f_pj%d" % i, [128, 512], F32) for i in range(2)]
            S.op("pool", lambda e: e.memset(upb[:], 0.0), writes=[upb])
            for t in range(NTL):
                tr = slice(t * 128, (t + 1) * 128)
                S.dma("sp", lambda e, tr=tr: e.dma_start(out=x2[:], in_=X2_d[tr, :]), reads=[X2_d], writes=[x2])
                S.op("act", lambda e: e.activation(out=x2b[:], in_=x2[:], func=AF.Copy), reads=[x2], writes=[x2b])
                transposes(x2b, 8, 128, tp, x2T)
                for j in range(22):
                    pu_, pg_ = pu[j % 2], pg[j % 2]
                    for k in range(8):
                        S.op("pe", lambda e, j=j, k=k, pu_=pu_: e.matmul(pu_[:], lhsT=Wu[:, k, j * 128:(j + 1) * 128], rhs=x2T[:, k, :], start=(k == 0), stop=(k == 7)),
                             reads=[Wu, x2T], writes=[pu_])
                    for k in range(8):
                        S.op("pe", lambda e, j=j, k=k, pg_=pg_: e.matmul(pg_[:], lhsT=Wu[:, k, DFF + j * 128:DFF + (j + 1) * 128], rhs=x2T[:, k, :], start=(k == 0), stop=(k == 7)),
                             reads=[Wu, x2T], writes=[pg_])
                    S.op("act", lambda e, j=j, pu_=pu_: e.activation(out=upb[:, j, 2:130], in_=pu_[:], func=AF.Copy), reads=[pu_], writes=[upb])
                    S.op("dve", lambda e, j=j: e.tensor_scalar(out=cv[:], in0=upb[:, j, 2:130], scalar1=cw[:, 2, j:j + 1], scalar2=cb[:, j:j + 1], op0=ALU.mult, op1=ALU.add),
                         reads=[upb, cw, cb], writes=[cv])
                    S.op("dve", lambda e, j=j: e.scalar_tensor_tensor(out=cv[:], in0=upb[:, j, 1:129], scalar=cw[:, 1, j:j + 1], in1=cv[:], op0=ALU.mult, op1=ALU.add),
                         reads=[upb, cw, cv], writes=[cv])
                    S.op("dve", lambda e, j=j: e.scalar_tensor_tensor(out=cv[:], in0=upb[:, j, 0:128], scalar=cw[:, 0, j:j + 1], in1=cv[:], op0=ALU.mult, op1=ALU.add),
                         reads=[upb, cw, cv], writes=[cv])
                    S.op("act", lambda e: e.activation(out=ge[:], in_=cv[:], func=AF.Gelu_apprx_tanh), reads=[cv], writes=[ge])
                    S.op("dve", lambda e, j=j, pg_=pg_: e.tensor_tensor(out=hT[:, j, :], in0=ge[:], in1=pg_[:], op=ALU.mult), reads=[ge, pg_], writes=[hT])
                S.op("pool", lambda e: e.tensor_copy(out=upb[:, :, 0:2], in_=upb[:, :, 128:130]), reads=[upb], writes=[upb])
                for g in range(2):
                    for j in range(22):
                        S.op("pe", lambda e, g=g, j=j: e.matmul(pj[g][:], lhsT=hT[:, j, :], rhs=Wd[:, j, g * 512:(g + 1) * 512], start=(j == 0), stop=(j == 21)),
                             reads=[hT, Wd], writes=[pj[g]])
                    S.op("dve", lambda e, g=g: e.scalar_tensor_tensor(out=zz[:, g * 512:(g + 1) * 512], in0=x2[:, g * 512:(g + 1) * 512], scalar=ALPHA, in1=pj[g][:], op0=ALU.mult, op1=ALU.add),
                         reads=[x2, pj[g]], writes=[zz])
                layernorm(zz, x2, 128, g3, b3, st, mv_, rs)
                S.dma("sp", lambda e, tr=tr: e.dma_start(out=xdst_ap[tr, :], in_=x2[:]), reads=[x2], writes=[xdst_b])
            for j in range(22):
                S.op("pe", lambda e, j=j: e.transpose(out=pj[0][0:2, (j % 4) * 128:(j % 4 + 1) * 128], in_=upb[:, j, 0:2], identity=identf[:]),
                     reads=[upb, identf], writes=[pj[0]])
                if j % 4 == 3 or j == 21:
                    j0 = (j // 4) * 4
                    S.op("dve", lambda e, j=j, j0=j0: e.tensor_copy(out=cvo[:, j0 * 128:(j + 1) * 128], in_=pj[0][0:2, 0:(j - j0 + 1) * 128]), reads=[pj[0]], writes=[cvo])
            S.dma("sp", lambda e: e.dma_start(out=cv_p[l], in_=cvo[:]), reads=[cvo])


    def phase_A1():
        with S.phase():
            W = S.sb("W1", [128, 8, 4104], BF16)
            load_w_bf16(W, od_w_in, D, 4104)
            cng_bc = S.sb("cng_bc", [128, 128], F32); dng_bc = S.sb("dng_bc", [128, 128], F32)
            bif_bc = S.sb("bif_bc", [128, 8], F32)
            rg = S.sb("rg", [128, 12], F32); rdec = S.sb("rdec", [128, 8, 2], F32)
            bcast_load(cng_bc, od_cng, 128); bcast_load(dng_bc, od_dng, 128); bcast_load(bif_bc, od_b_if, 8)
            S.dma("sp", lambda e: e.dma_start(out=rg[:], in_=retg), writes=[rg])
            S.dma("sp", lambda e: e.dma_start(out=rdec[:].rearrange("p a b -> p (a b)"), in_=retdec), writes=[rdec])
            xin = S.sb("a_xin", [128, D], F32); xb = S.sb("a_xb", [128, D], BF16); xT = S.sb("a_xT", [128, 8, 128], BF16)
            rope = S.sb("a_rope", [128, 256], F32)
            praw = S.sb("a_praw", [128, 512], F32)
            tt = [S.sb("a_t%d" % i, [128, 256], F32) for i in range(4)]
            r32 = S.sb("a_r32", [128, 512], F32)
            qin = S.sb("a_qin", [128, 512], BF16); kin = S.sb("a_kin", [128, 512], BF16); kend = S.sb("a_kend", [128, 512], BF16)
            vh = S.sb("a_vh", [128, 4, 128], BF16)
            gate = S.sb("a_gate", [128, 512], F32)
            qT = S.sb("a_qT", [128, 4, 128], BF16); kT = S.sb("a_kT", [128, 4, 128], BF16)
            atm = S.sb("a_atm", [128, 4, 64], BF16)
            Stc = S.sb("a_Stc", [128, 4, 128], F32); Stcb = S.sb("a_Stcb", [128, 4, 128], BF16)
            Std = S.sb("a_Std", [128, 4, 130], F32); Stdb = S.sb("a_Stdb", [128, 4, 130], BF16)
            vaug = S.sb("a_vaug", [128, 4, 130], BF16)
            on32 = S.sb("a_on32", [128, 4, 128], F32); sqq = S.sb("a_sqq", [128, 4, 128], F32)
            hh = S.sb("a_hh", [128, 4, 128], F32)
            ss = S.sb("a_ss", [128, 4], F32); rs = S.sb("a_rs", [128, 4], F32)
            omb = S.sb("a_omb", [128, 512], BF16)
            gif = S.sb("a_gif", [128, 8], F32); lft = S.sb("a_lft", [128, 4], F32)
            fm = S.sb("a_fm", [4, 12, 128], F32)
            Bc = S.sb("a_Bc", [4, 1], F32); Gc = S.sb("a_Gc", [4, 1], F32); mfin = S.sb("a_mfin", [4, 1], F32)
            EX = S.sb("a_EX", [128, 4, 4], F32)
            decD = S.sb("a_decD", [128, 8, 2], F32)
            dd = S.sb("a_dd", [128, 4], F32)
            tp = S.ps("a_tp", [128, 8, 128], BF16)
            pj = S.ps("a_pj", [128, 512], F32)
            pat = S.ps("a_pat", [128, 4, 64], F32)
            po = S.ps("a_po", [128, 4, 256], F32)
            pds = S.ps("a_pds", [128, 4, 256], F32)
            psm = S.ps("a_psm", [128, 512], F32)
            pif = psm; pTv = psm; pVv = psm; pdd = psm
            for b_ in (Stc, Stcb, Std, Stdb, Bc, Gc):
                S.op("pool", lambda e, b_=b_: e.memset(b_[:], 0.0), writes=[b_])
            S.op("pool", lambda e: e.memset(vaug[:], 1.0), writes=[vaug])

            def do_rope(src, dst, c0, nm, f):
                sv = src[:].rearrange("p (m two f) -> p m two f", two=2, f=f)
                dv = dst[:].rearrange("p (m two f) -> p m two f", two=2, f=f)
                cb = rope[:, c0:c0 + f].unsqueeze(1).to_broadcast([128, nm, f])
                sb_ = rope[:, c0 + f:c0 + 2 * f].unsqueeze(1).to_broadcast([128, nm, f])
                t1, t2, t3, t4 = [t_[:].rearrange("p (m f) -> p m f", f=f) for t_ in tt]
                S.op("dve", lambda e: e.tensor_tensor(out=t1, in0=sv[:, :, 0, :], in1=cb, op=ALU.mult), reads=[src, rope], writes=[tt[0]])
                S.op("pool", lambda e: e.tensor_tensor(out=t2, in0=sv[:, :, 1, :], in1=sb_, op=ALU.mult), reads=[src, rope], writes=[tt[1]])
                S.op("dve", lambda e: e.tensor_tensor(out=dv[:, :, 0, :], in0=t1, in1=t2, op=ALU.subtract), reads=[tt[0], tt[1]], writes=[dst])
                S.op("pool", lambda e: e.tensor_tensor(out=t3, in0=sv[:, :, 1, :], in1=cb, op=ALU.mult), reads=[src, rope], writes=[tt[2]])
                S.op("dve", lambda e: e.tensor_tensor(out=t4, in0=sv[:, :, 0, :], in1=sb_, op=ALU.mult), reads=[src, rope], writes=[tt[3]])
                S.op("pool", lambda e: e.tensor_tensor(out=dv[:, :, 1, :], in0=t3, in1=t4, op=ALU.add), reads=[tt[2], tt[3]], writes=[dst])

            def hb(buf, c0):
                return buf[:, c0:c0 + 4].unsqueeze(2).to_broadcast([128, 4, 128])

            def v3(buf):
                return buf[:].rearrange("p (h e) -> p h e", h=4)

            for t in range(NTL):
                tr = slice(t * 128, (t + 1) * 128)
                S.dma("sp", lambda e, tr=tr: e.dma_start(out=xin[:], in_=X_d[tr, :]), reads=[X_d], writes=[xin])
                S.dma("sp", lambda e, tr=tr: e.dma_start(out=rope[:], in_=ropeC[tr, :]), writes=[rope])
                S.op("act", lambda e: e.activation(out=xb[:], in_=xin[:], func=AF.Copy), reads=[xin], writes=[xb])
                transposes(xb, 8, 128, tp, xT)
                proj_tm(pj, xT, W, 8, 0, 512, 128)
                S.op("act", lambda e: e.activation(out=praw[:], in_=pj[:], func=AF.Copy), reads=[pj], writes=[praw])
                do_rope(praw, r32, 0, 4, 64)
                S.op("dve", lambda e: e.tensor_tensor(out=v3(qin), in0=v3(r32), in1=hb(rg, 0), op=ALU.mult), reads=[r32, rg], writes=[qin])
                proj_tm(pj, xT, W, 8, 512, 512, 128)
                S.op("act", lambda e: e.activation(out=praw[:], in_=pj[:], func=AF.Copy), reads=[pj], writes=[praw])
                do_rope(praw, r32, 0, 4, 64)
                S.op("dve", lambda e: e.tensor_tensor(out=v3(kin), in0=v3(r32), in1=hb(rg, 4), op=ALU.mult), reads=[r32, rg], writes=[kin])
                S.op("pool", lambda e: e.tensor_tensor(out=v3(kend), in0=v3(r32), in1=hb(rg, 8), op=ALU.mult), reads=[r32, rg], writes=[kend])
                proj_tm(pj, xT, W, 8, 1024, 512, 128)
                S.op("act", lambda e: e.activation(out=vh[:].rearrange("p h e -> p (h e)"), in_=pj[:], func=AF.Copy), reads=[pj], writes=[vh])
                proj_tm(pj, xT, W, 8, 1536, 512, 128)
                S.op("act", lambda e: e.activation(out=gate[:], in_=pj[:], func=AF.Silu), reads=[pj], writes=[gate])
                transposes(qin, 4, 128, tp, qT)
                transposes(kin, 4, 128, tp, kT)
                gla_tile(qT, kT, kend, vh, 128, rdec, Stc, Stcb, pat, po, pds, atm)
                head_norm(po, po[:, :, 0:128], on32, 128, sqq, ss, rs, center=True)
                S.op("pool", lambda e: e.tensor_tensor(out=on32[:], in0=on32[:], in1=cng_bc[:].unsqueeze(1).to_broadcast([128, 4, 128]), op=ALU.mult),
                     reads=[on32, cng_bc], writes=[on32])
                S.op("dve", lambda e: e.tensor_tensor(out=omb[:], in0=on32[:].rearrange("p h e -> p (h e)"), in1=gate[:], op=ALU.mult),
                     reads=[on32, gate], writes=[omb])
                S.dma("sp", lambda e, tr=tr: e.dma_start(out=OMIX_d[tr, 0:512], in_=omb[:]), reads=[omb], writes=[OMIX_d])
                proj_tm(pif, xT, W, 8, 4096, 8, 128)
                S.op("dve", lambda e: e.tensor_tensor(out=gif[:], in0=pif[:, 0:8], in1=bif_bc[:], op=ALU.add), reads=[pif, bif_bc], writes=[gif])
                S.op("act", lambda e: e.activation(out=lft[:], in_=gif[:, 4:8], func=AF.Sigmoid), reads=[gif], writes=[lft])
                S.op("act", lambda e: e.activation(out=lft[:], in_=lft[:], func=AF.Ln), reads=[lft], writes=[lft])
                S.op("pe", lambda e: e.transpose(out=pTv[0:4, 0:128], in_=gif[:, 0:4], identity=identf[:]), reads=[gif, identf], writes=[psm])
                S.op("pe", lambda e: e.transpose(out=pTv[0:4, 128:256], in_=lft[:], identity=identf[:]), reads=[lft, identf], writes=[psm])
                S.op("dve", lambda e: e.tensor_copy(out=fm[:, 0:2, :], in_=pTv[0:4, 0:256].rearrange("p (a n) -> p a n", a=2)), reads=[psm], writes=[fm])
                liT, lfT, Bv, Av, Gv, Gp, GL = [fm[:, i, :] for i in range(7)]
                S.op("dve", lambda e: e.tensor_tensor_scan(out=Bv, data0=onesf[0:4, :], data1=lfT, initial=Bc[:, 0:1], op0=ALU.mult, op1=ALU.add),
                     reads=[fm, onesf, Bc], writes=[fm])
                S.op("dve", lambda e: e.tensor_tensor(out=Av, in0=liT, in1=Bv, op=ALU.subtract), reads=[fm], writes=[fm])
                S.op("dve", lambda e: e.tensor_tensor_scan(out=Gv, data0=Av, data1=Av, initial=Gc[:, 0:1], op0=ALU.max, op1=ALU.max),
                     reads=[fm, Gc], writes=[fm])
                S.op("dve", lambda e: e.tensor_scalar(out=fm[:, 5, 0:64], in0=zerosf[0:4, 0:64], scalar1=Gc[:, 0:1], scalar2=None, op0=ALU.add), reads=[Gc, zerosf], writes=[fm])
                S.op("dve", lambda e: e.tensor_scalar(out=fm[:, 5, 64:128], in0=zerosf[0:4, 0:64], scalar1=fm[:, 4, 63:64], scalar2=None, op0=ALU.add), reads=[fm, zerosf], writes=[fm])
                S.op("dve", lambda e: e.tensor_scalar(out=fm[:, 6, 0:64], in0=zerosf[0:4, 0:64], scalar1=fm[:, 4, 63:64], scalar2=None, op0=ALU.add), reads=[fm, zerosf], writes=[fm])
                S.op("dve", lambda e: e.tensor_scalar(out=fm[:, 6, 64:128], in0=zerosf[0:4, 0:64], scalar1=fm[:, 4, 127:128], scalar2=None, op0=ALU.add), reads=[fm, zerosf], writes=[fm])
                S.op("dve", lambda e: e.tensor_tensor(out=fm[:, 7, :], in0=Av, in1=Gp, op=ALU.subtract), reads=[fm], writes=[fm])
                S.op("dve", lambda e: e.tensor_tensor(out=fm[:, 8, :], in0=Av, in1=GL, op=ALU.subtract), reads=[fm], writes=[fm])
                S.op("dve", lambda e: e.tensor_tensor(out=fm[:, 9, :], in0=Gp, in1=Gv, op=ALU.subtract), reads=[fm], writes=[fm])
                S.op("dve", lambda e: e.scalar_tensor_tensor(out=fm[:, 10, :], in0=Bv, scalar=-1.0, in1=Gv, op0=ALU.mult, op1=ALU.subtract), reads=[fm], writes=[fm])
                S.op("dve", lambda e: e.tensor_copy(out=Bc[:], in_=fm[:, 2, 127:128]), reads=[fm], writes=[Bc])
                S.op("dve", lambda e: e.tensor_copy(out=Gc[:], in_=fm[:, 4, 127:128]), reads=[fm], writes=[Gc])
                for i in range(4):
                    S.op("pe", lambda e, i=i: e.transpose(out=pVv[:, 256 + i * 4:256 + i * 4 + 4], in_=fm[:, 7 + i, :], identity=identf[0:4, 0:4]),
                         reads=[fm, identf], writes=[psm])
                S.op("act", lambda e: e.activation(out=EX[:].rearrange("p a b -> p (a b)"), in_=pVv[:, 256:272], func=AF.Exp), reads=[psm], writes=[EX])
                for c in range(2):
                    S.op("pe", lambda e, c=c: e.matmul(pdd[:, 272 + c * 4:272 + c * 4 + 4], lhsT=SelC[:, c, :], rhs=EX[:, 2, :], start=True, stop=True),
                         reads=[SelC, EX], writes=[psm])
                S.op("dve", lambda e: e.tensor_copy(out=decD[:, :, 0:1], in_=pdd[:, 272:280].unsqueeze(2)), reads=[psm], writes=[decD])
                exb = lambda i: EX[:, i, :].unsqueeze(2).to_broadcast([128, 4, 128])
                proj_tm(pj, xT, W, 8, 2048, 512, 128)
                S.op("act", lambda e: e.activation(out=praw[:], in_=pj[:], func=AF.Copy, scale=128.0 ** -0.5), reads=[pj], writes=[praw])
                S.op("dve", lambda e: e.tensor_tensor(out=v3(qin), in0=v3(praw), in1=exb(2), op=ALU.mult), reads=[praw, EX], writes=[qin])
                proj_tm(pj, xT, W, 8, 2560, 512, 128)
                S.op("act", lambda e: e.activation(out=praw[:], in_=pj[:], func=AF.Copy), reads=[pj], writes=[praw])
                S.op("dve", lambda e: e.tensor_tensor(out=v3(kin), in0=v3(praw), in1=exb(0), op=ALU.mult), reads=[praw, EX], writes=[kin])
                S.op("pool", lambda e: e.tensor_tensor(out=v3(kend), in0=v3(praw), in1=exb(1), op=ALU.mult), reads=[praw, EX], writes=[kend])
                proj_tm(pj, xT, W, 8, 3072, 512, 128)
                S.op("act", lambda e: e.activation(out=vaug[:, :, 0:128], in_=pj[:].rearrange("p (h e) -> p h e", h=4), func=AF.Copy), reads=[pj], writes=[vaug])
                proj_tm(pj, xT, W, 8, 3584, 512, 128)
                S.op("act", lambda e: e.activation(out=gate[:], in_=pj[:], func=AF.Sigmoid), reads=[pj], writes=[gate])
                transposes(qin, 4, 128, tp, qT)
                transposes(kin, 4, 128, tp, kT)
                gla_tile(qT, kT, kend, vaug, 129, decD, Std, Stdb, pat, po, pds, atm)
                S.op("dve", lambda e: e.tensor_scalar(out=dd[:], in0=po[:, :, 128], scalar1=-1.0, scalar2=None, op0=ALU.mult), reads=[po], writes=[dd])
                S.op("dve", lambda e: e.tensor_tensor(out=dd[:], in0=po[:, :, 128], in1=dd[:], op=ALU.max), reads=[po, dd], writes=[dd])
                S.op("dve", lambda e: e.tensor_tensor(out=dd[:], in0=dd[:], in1=EX[:, 3, :], op=ALU.max), reads=[dd, EX], writes=[dd])
                S.op("dve", lambda e: e.reciprocal(out=dd[:], in_=dd[:]), reads=[dd], writes=[dd])
                S.op("dve", lambda e: e.tensor_tensor(out=hh[:], in0=po[:, :, 0:128], in1=dd[:].unsqueeze(2).to_broadcast([128, 4, 128]), op=ALU.mult),
                     reads=[po, dd], writes=[hh])
                head_norm(hh, hh[:, :, :], on32, 128, sqq, ss, rs, center=True)
                S.op("pool", lambda e: e.tensor_tensor(out=on32[:], in0=on32[:], in1=dng_bc[:].unsqueeze(1).to_broadcast([128, 4, 128]), op=ALU.mult),
                     reads=[on32, dng_bc], writes=[on32])
                S.op("dve", lambda e: e.tensor_tensor(out=omb[:], in0=on32[:].rearrange("p h e -> p (h e)"), in1=gate[:], op=ALU.mult),
                     reads=[on32, gate], writes=[omb])
                S.dma("sp", lambda e, tr=tr: e.dma_start(out=OMIX_d[tr, 512:1024], in_=omb[:]), reads=[omb], writes=[OMIX_d])
            S.op("dve", lambda e: e.tensor_tensor(out=mfin[:], in0=Bc[:], in1=Gc[:], op=ALU.add), reads=[Bc, Gc], writes=[mfin])
            S.dma("sp", lambda e: e.dma_start(out=dm_p, in_=mfin[:]), reads=[mfin])
            S.dma("sp", lambda e: e.dma_start(out=sc_p.rearrange("h d e -> d h e"), in_=Stc[:]), reads=[Stc])
            S.dma("sp", lambda e: e.dma_start(out=dc_p.rearrange("h d e -> d h e"), in_=Std[:, :, 0:128]), reads=[Std])
            S.dma("sp", lambda e: e.dma_start(out=dn_p.rearrange("h d -> d h"), in_=Std[:, :, 128], allow_slow_non_contiguous=True), reads=[Std])


    xs = din("xs", [NSEQ, D])
    ropeSA = din("ropeSA", [NSEQ, 128])
    ropeSC = din("ropeSC", [NSEQ, 256])
    kpool = din("kpool", [NPOOL * 128, 512])
    vpool = din("vpool", [NPOOL * 128, 512])
    ptab = din("ptab", [1, NSEQ * NPG], I32)
    st_b = din("st_b", [NSEQ, 4, 128, 128])
    st_c = din("st_c", [NSEQ, 4, 128, 128])
    st_dc = din("st_dc", [NSEQ, 4, 128, 128])
    st_dn = din("st_dn", [NSEQ, 512])
    st_dm = din("st_dm", [NSEQ, 4])
    cmk = din("cmk", [2, NSEQ, NMEM, D])
    cmv = din("cmv", [2, NSEQ, NMEM, D])
    sconv = din("sconv", [2, NSEQ * 2, DFF])
    y_s = dout("y_s", [NSEQ, D])
    ak_s = dout("ak_s", [NSEQ, 512])
    av_s = dout("av_s", [NSEQ, 512])
    sb_s = dout("sb_s", [NSEQ, 4, 128, 128])
    sc_s = dout("sc_s", [NSEQ, 4, 128, 128])
    dc_s = dout("dc_s", [NSEQ, 4, 128, 128])
    dn_s = dout("dn_s", [NSEQ, 512])
    dm_s = dout("dm_s", [NSEQ, 4])
    cv_s = dout("cv_s", [2, NSEQ, 2, DFF])
    XS_d = S.dram("XS_d", [NSEQ, D], F32)
    XS2_d = S.dram("XS2_d", [NSEQ, D], F32)
    OMS_d = S.dram("OMS_d", [NSEQ, D], BF16)
    SP_d = S.dram("SP_d", [8, NSEQ, 512], F32)
    SG_d = S.dram("SG_d", [128, 128], F32)
    SP1_d = S.dram("SP1_d", [9, NSEQ, 512], F32)
    NL_d = S.dram("NL_d", [128, 1], F32)
    xs_b = S.ext("xs", xs)
    ys_b = S.ext("y_s", y_s)

    OneH = S.sb("OneH", [16, 16, 128], F32)
    Dlt = S.sb("Dlt", [128, 16, 16], F32)
    SelS = S.sb("SelS", [8, 16, 16], F32)
    bm8 = S.sb("bm8", [8, 512], F32)
    bm4 = S.sb("bm4", [4, 1024], F32)
    hi4 = S.sb("hi4", [8, 1], F32)
    S.op("pool", lambda e: e.memset(OneH[:], 1.0), writes=[OneH])
    S.op("pool", lambda e: e.affine_select(out=OneH[:], in_=OneH[:], pattern=[[1, 16], [0, 128]], compare_op=ALU.is_equal, fill=0.0, base=0, channel_multiplier=-1),
         reads=[OneH], writes=[OneH])
    S.op("pool", lambda e: e.memset(Dlt[:], 1.0), writes=[Dlt])
    S.op("pool", lambda e: e.affine_select(out=Dlt[:], in_=Dlt[:], pattern=[[1, 16], [-1, 16]], compare_op=ALU.is_equal, fill=0.0, base=0, channel_multiplier=0),
         reads=[Dlt], writes=[Dlt])
    S.op("pool", lambda e: e.memset(SelS[:], 1.0), writes=[SelS])
    S.op("pool", lambda e: e.affine_select(out=SelS[:], in_=SelS[:], pattern=[[1, 16], [-1, 16]], compare_op=ALU.is_equal, fill=0.0, base=0, channel_multiplier=0),
         reads=[SelS], writes=[SelS])
    S.op("pool", lambda e: e.memset(bm8[:], 1.0), writes=[bm8])
    S.op("pool", lambda e: e.memset(bm4[:], 1.0), writes=[bm4])
    S.op("pool", lambda e: e.memset(hi4[:], 1.0), writes=[hi4])
    S.op("pool", lambda e: e.affine_select(out=hi4[:], in_=hi4[:], pattern=[[0, 1]], compare_op=ALU.is_ge, fill=0.0, base=-4, channel_multiplier=1), reads=[hi4], writes=[hi4])
    for (bm_, wd, nrow) in ((bm4, 256, 4),):
        S.op("pool", lambda e, bm_=bm_, wd=wd: e.affine_select(out=bm_[:], in_=bm_[:], pattern=[[1, 4 * wd]], compare_op=ALU.is_ge, fill=0.0, base=0, channel_multiplier=-wd), reads=[bm_], writes=[bm_])
        S.op("pool", lambda e, bm_=bm_, wd=wd: e.affine_select(out=bm_[:], in_=bm_[:], pattern=[[-1, 4 * wd]], compare_op=ALU.is_ge, fill=0.0, base=wd - 1, channel_multiplier=wd), reads=[bm_], writes=[bm_])
    bm8b = S.sb("bm8b", [8, 512], F32)
    S.op("pool", lambda e: e.memset(bm8b[:], 1.0), writes=[bm8b])
    S.op("pool", lambda e: e.affine_select(out=bm8[:], in_=bm8[:], pattern=[[1, 512]], compare_op=ALU.is_ge, fill=0.0, base=0, channel_multiplier=-128), reads=[bm8], writes=[bm8])
    S.op("pool", lambda e: e.affine_select(out=bm8[:], in_=bm8[:], pattern=[[-1, 512]], compare_op=ALU.is_ge, fill=0.0, base=127, channel_multiplier=128), reads=[bm8], writes=[bm8])
    S.op("pool", lambda e: e.affine_select(out=bm8b[:], in_=bm8b[:], pattern=[[1, 512]], compare_op=ALU.is_ge, fill=0.0, base=512, channel_multiplier=-128), reads=[bm8b], writes=[bm8b])
    S.op("pool", lambda e: e.affine_select(out=bm8b[:], in_=bm8b[:], pattern=[[-1, 512]], compare_op=ALU.is_ge, fill=0.0, base=127 - 512, channel_multiplier=128), reads=[bm8b], writes=[bm8b])
    S.op("pool", lambda e: e.tensor_tensor(out=bm8[:], in0=bm8[:], in1=bm8b[:], op=ALU.add), reads=[bm8, bm8b], writes=[bm8])

    def rope_rows(src, dst, ropeb, c0, nm, f, P, tmps):
        sv = src[0:P, :].rearrange("p (m two f) -> p m two f", two=2, f=f)
        dv = dst[0:P, :].rearrange("p (m two f) -> p m two f", two=2, f=f)
        cb = ropeb[0:P, c0:c0 + f].unsqueeze(1).to_broadcast([P, nm, f])
        sb_ = ropeb[0:P, c0 + f:c0 + 2 * f].unsqueeze(1).to_broadcast([P, nm, f])
        t1, t2, t3, t4 = [t_[0:P, :].rearrange("p (m f) -> p m f", f=f) for t_ in tmps]
        S.op("dve", lambda e: e.tensor_tensor(out=t1, in0=sv[:, :, 0, :], in1=cb, op=ALU.mult), reads=[src, ropeb], writes=[tmps[0]])
        S.op("pool", lambda e: e.tensor_tensor(out=t2, in0=sv[:, :, 1, :], in1=sb_, op=ALU.mult), reads=[src, ropeb], writes=[tmps[1]])
        S.op("dve", lambda e: e.tensor_tensor(out=dv[:, :, 0, :], in0=t1, in1=t2, op=ALU.subtract), reads=[tmps[0], tmps[1]], writes=[dst])
        S.op("pool", lambda e: e.tensor_tensor(out=t3, in0=sv[:, :, 1, :], in1=cb, op=ALU.mult), reads=[src, ropeb], writes=[tmps[2]])
        S.op("dve", lambda e: e.tensor_tensor(out=t4, in0=sv[:, :, 0, :], in1=sb_, op=ALU.mult), reads=[src, ropeb], writes=[tmps[3]])
        S.op("pool", lambda e: e.tensor_tensor(out=dv[:, :, 1, :], in0=t3, in1=t4, op=ALU.add), reads=[tmps[2], tmps[3]], writes=[dst])

    def to_fm(src32, dst, tpf):
        transposes(src32, 4, NSEQ, tpf, dst, ident=identf)

    def bcast_rows_fm(val, dstb, X, pbc):
        S.op("dve", lambda e: e.tensor_tensor(out=X[:], in0=val[0:16, 0:4].unsqueeze(2).to_broadcast([16, 4, 16]),
                                              in1=identf[0:16, 0:16].unsqueeze(1).to_broadcast([16, 4, 16]), op=ALU.mult), reads=[val, identf], writes=[X])
        S.op("pe", lambda e: e.matmul(pbc[:, 0:64], lhsT=onesf[0:16, :], rhs=X[:].rearrange("k h i -> k (h i)"), start=True, stop=True), reads=[onesf, X], writes=[pbc])
        S.op("dve", lambda e: e.tensor_copy(out=dstb[:].rearrange("p h i -> p (h i)"), in_=pbc[:, 0:64]), reads=[pbc], writes=[dstb])

    def state_steps(S_in, S_out, vTM, qT, kcol, fcol, pO, pvb, Sb, tmpb, Qz):
        for h in range(4):
            S.op("dve", lambda e, h=h: e.tensor_tensor(out=Qz[:, h, :, :], in0=qT[:, h, :].unsqueeze(2).to_broadcast([128, 16, 16]), in1=Dlt[:], op=ALU.mult),
                 reads=[qT, Dlt], writes=[Qz])
        n = 0
        for i in range(NSEQ):
            S.op("pe", lambda e, i=i: e.matmul(pvb[:], lhsT=OneH[:, i, :], rhs=vTM[0:16, :], start=True, stop=True), reads=[OneH, vTM], writes=[pvb])
            for h in range(4):
                sb_ = Sb[n % 2]; tb_ = tmpb[n % 2]
                S.dma("sp", lambda e, i=i, h=h, sb_=sb_: e.dma_start(out=sb_[:], in_=S_in[i, h]), writes=[sb_])
                S.op("dve", lambda e, h=h, i=i, tb_=tb_: e.tensor_scalar(out=tb_[:], in0=pvb[:, h * 128:(h + 1) * 128], scalar1=kcol(h, i), scalar2=None, op0=ALU.mult),
                     reads=[pvb] + kcol.bufs, writes=[tb_])
                S.op("dve", lambda e, h=h, i=i, sb_=sb_, tb_=tb_: e.scalar_tensor_tensor(out=sb_[:], in0=sb_[:], scalar=fcol(h, i), in1=tb_[:], op0=ALU.mult, op1=ALU.add),
                     reads=[sb_, tb_] + fcol.bufs, writes=[sb_])
                S.dma("sp", lambda e, i=i, h=h, sb_=sb_: e.dma_start(out=S_out[i, h], in_=sb_[:]), reads=[sb_])
                S.op("pe", lambda e, i=i, h=h, sb_=sb_, n=n: e.matmul(pO[0:16, h * 128:(h + 1) * 128], lhsT=Qz[:, h, i, :], rhs=sb_[:], start=(n == 0), stop=(n == 4 * NSEQ - 1),
                                                                      skip_group_check=True), reads=[Qz, sb_], writes=[pO])
                n += 1

    class Col:
        def __init__(self, fn, bufs):
            self.fn, self.bufs = fn, list(bufs)

        def __call__(self, h, i):
            return self.fn(h, i)

    def dec_attend(i, qTM, width, G, dg, Kb, Vb, npg, perm_c, mb, selc, bm, pqb, psT, pPT, pOut, pAcc, scb, scT, PTb, outm, mx, ll):
        for c0 in range(0, width, 512):
            S.op("pe", lambda e, c0=c0: e.matmul(pqb[:, c0:c0 + 512], lhsT=OneH[:, i, :], rhs=qTM[0:16, c0:c0 + 512], start=True, stop=True), reads=[OneH, qTM], writes=[pqb])
        S.op("dve", lambda e: e.tensor_tensor(out=Kb[:, 0:npg, :], in0=Kb[:, 0:npg, :], in1=pqb[:, 0:width].unsqueeze(1).to_broadcast([128, npg, width]), op=ALU.mult),
             reads=[Kb, pqb], writes=[Kb])
        kv = Kb[:, 0:npg, :].rearrange("p g (a d) -> p g a d", d=dg)
        if perm_c:
            ov = scb[:, 0:npg, :].rearrange("p g (c h) -> p g h c", c=2)
            kv = Kb[:, 0:npg, :].rearrange("p g (h c d) -> p g h c d", c=2, d=dg)
        else:
            ov = scb[:, 0:npg, :]
        S.op("dve", lambda e: e.tensor_reduce(out=ov, in_=kv, axis=AX.X, op=ALU.add), reads=[Kb], writes=[scb])
        for g0 in range(0, npg, 4):
            ng = min(4, npg - g0)
            pt_ = psT[(g0 // 4) % 2]
            for g in range(ng):
                S.op("pe", lambda e, g=g, g0=g0, pt_=pt_: e.transpose(out=pt_[0:G, g * 128:(g + 1) * 128], in_=scb[:, g0 + g, :], identity=identf[:]),
                     reads=[scb, identf], writes=[pt_])
            S.op("dve", lambda e, g0=g0, ng=ng, pt_=pt_: e.tensor_copy(out=scT[0:G, g0 * 128:(g0 + ng) * 128], in_=pt_[0:G, 0:ng * 128]), reads=[pt_], writes=[scT])
        nk = npg * 128
        if mb is not None:
            S.op("dve", lambda e: e.tensor_tensor(out=scT[0:G, nk - 128:nk], in0=scT[0:G, nk - 128:nk], in1=mb[0:G, :], op=ALU.add), reads=[scT, mb], writes=[scT])
        S.op("dve", lambda e: e.tensor_reduce(out=mx[0:G, :], in_=scT[0:G, 0:nk], axis=AX.X, op=ALU.max, negate=True), reads=[scT], writes=[mx])
        S.op("act", lambda e: e.activation(out=scT[0:G, 0:nk], in_=scT[0:G, 0:nk], func=AF.Exp, bias=mx[0:G, 0:1], scale=1.0, accum_out=ll[0:G, :]), reads=[scT, mx], writes=[scT, ll])
        S.op("dve", lambda e: e.reciprocal(out=ll[0:G, :], in_=ll[0:G, :]), reads=[ll], writes=[ll])
        for g in range(npg):
            S.op("pe", lambda e, g=g: e.transpose(out=pPT[:, g * 8:g * 8 + G], in_=scT[0:G, g * 128:(g + 1) * 128], identity=identf[0:G, 0:G]), reads=[scT, identf], writes=[pPT])
        S.op("dve", lambda e: e.tensor_copy(out=PTb[:, 0:npg, 0:G], in_=pPT[:, 0:npg * 8].rearrange("p (g a) -> p g a", a=8)[:, :, 0:G]), reads=[pPT], writes=[PTb])
        for c0 in range(0, width, 512):
            for g in range(npg):
                S.op("pe", lambda e, g=g, c0=c0: e.matmul(pOut[0:G, c0:c0 + 512], lhsT=PTb[:, g, 0:G], rhs=Vb[:, g, c0:c0 + 512], start=(g == 0), stop=(g == npg - 1)),
                     reads=[PTb, Vb], writes=[pOut])
        S.op("dve", lambda e: e.scalar_tensor_tensor(out=outm[0:G, 0:width], in0=pOut[0:G, 0:width], scalar=ll[0:G, 0:1], in1=bm[0:G, 0:width], op0=ALU.mult, op1=ALU.mult),
             reads=[pOut, ll, bm], writes=[outm])
        for c0 in range(0, width, 512):
            S.op("pe", lambda e, c0=c0: e.matmul(pAcc[0:16, c0:c0 + 512], lhsT=selc[0:G, i, :], rhs=outm[0:G, c0:c0 + 512], start=(i == 0), stop=(i == NSEQ - 1)),
                 reads=[selc, outm], writes=[pAcc])


    def sample_common(ph):
        b = {}
        b["xin"] = S.sb(ph + "xin", [NSEQ, D], F32); b["xb"] = S.sb(ph + "xb", [NSEQ, D], BF16); b["xT"] = S.sb(ph + "xT", [128, 8, NSEQ], BF16)
        b["tp"] = S.ps(ph + "tp", [128, 8, 128], BF16)
        b["tpf"] = S.ps(ph + "tpf", [128, 4, NSEQ], F32)
        b["pj"] = S.ps(ph + "pj", [128, 512], F32)
        b["pO"] = S.ps(ph + "pO", [128, 512], F32)
        b["pvb"] = S.ps(ph + "pvb", [128, 512], F32)
        b["tmps"] = [S.sb(ph + "t%d" % i, [NSEQ, 256], F32) for i in range(4)]
        b["praw"] = S.sb(ph + "praw", [NSEQ, 512], F32)
        b["Sb"] = [S.sb(ph + "Sb%d" % i, [128, 128], F32) for i in range(2)]
        b["tmpb"] = [S.sb(ph + "tb%d" % i, [128, 128], F32) for i in range(2)]
        b["Qz"] = S.sb(ph + "Qz", [128, 4, 16, 16], F32)
        b["on32"] = S.sb(ph + "on32", [NSEQ, 4, 128], F32); b["sqq"] = S.sb(ph + "sqq", [NSEQ, 4, 128], F32)
        b["ss"] = S.sb(ph + "ss", [NSEQ, 4], F32); b["rs"] = S.sb(ph + "rs", [NSEQ, 4], F32)
        b["omb"] = S.sb(ph + "omb", [NSEQ, D], BF16)
        b["gate"] = S.sb(ph + "gate", [NSEQ, 512], F32)
        return b

    def load_x_sample(b, src_b, src_ap):
        S.dma("sp", lambda e: e.dma_start(out=b["xin"][:], in_=src_ap), reads=[src_b], writes=[b["xin"]])
        S.op("act", lambda e: e.activation(out=b["xb"][:], in_=b["xin"][:], func=AF.Copy), reads=[b["xin"]], writes=[b["xb"]])
        transposes(b["xb"], 8, NSEQ, b["tp"], b["xT"])

    def phase_SA0():
        with S.phase():
            P = NSEQ
            W = S.sb("W0s", [128, 8, 3584], BF16)
            load_w_bf16(W, ev_w_in, D, 3584)
            b = sample_common("sa_")
            xT, pj, praw, tmps = b["xT"], b["pj"], b["praw"], b["tmps"]
            lbraw = S.sb("sa_lbraw", [128, 2, 512], F32); lb_bc = S.sb("sa_lb", [128, 512], F32); oml_bc = S.sb("sa_oml", [128, 512], F32)
            bng_bc = S.sb("sa_bng", [128, 128], F32); sg_bc = S.sb("sa_sg", [128, 128], F32)
            bcast_load(lbraw, ev_lb, 1024); bcast_load(bng_bc, ev_bng, 128); bcast_load(sg_bc, ev_subln_g, 128)
            S.op("act", lambda e: e.activation(out=lbraw[:], in_=lbraw[:], func=AF.Exp), reads=[lbraw], writes=[lbraw])
            S.op("dve", lambda e: e.tensor_tensor(out=oml_bc[:], in0=lbraw[:, 0, :], in1=lbraw[:, 1, :], op=ALU.add), reads=[lbraw], writes=[oml_bc])
            S.op("dve", lambda e: e.reciprocal(out=oml_bc[:], in_=oml_bc[:]), reads=[oml_bc], writes=[oml_bc])
            S.op("dve", lambda e: e.tensor_tensor(out=lb_bc[:], in0=lbraw[:, 0, :], in1=oml_bc[:], op=ALU.mult), reads=[lbraw, oml_bc], writes=[lb_bc])
            S.op("dve", lambda e: e.tensor_tensor(out=oml_bc[:], in0=lbraw[:, 1, :], in1=oml_bc[:], op=ALU.mult), reads=[lbraw, oml_bc], writes=[oml_bc])
            lamraw = S.sb("sa_lamraw", [128, 256], F32); lamp = S.sb("sa_lamp", [128, 2, 64], F32); lams = S.sb("sa_lams", [128, 2], F32); nlam = S.sb("sa_nlam", [128, 1], F32)
            bcast_load(lamraw, ev_lam, 256)
            lv = lamraw[:].rearrange("p (a b d) -> p a b d", a=2, b=2)
            S.op("dve", lambda e: e.tensor_tensor(out=lamp[:], in0=lv[:, :, 0, :], in1=lv[:, :, 1, :], op=ALU.mult), reads=[lamraw], writes=[lamp])
            S.op("dve", lambda e: e.tensor_reduce(out=lams[:], in_=lamp[:], axis=AX.X, op=ALU.add), reads=[lamp], writes=[lams])
            S.op("act", lambda e: e.activation(out=lams[:], in_=lams[:], func=AF.Exp), reads=[lams], writes=[lams])
            S.op("dve", lambda e: e.tensor_tensor(out=nlam[:], in0=lams[:, 1:2], in1=lams[:, 0:1], op=ALU.subtract), reads=[lams], writes=[nlam])
            S.op("dve", lambda e: e.tensor_scalar(out=nlam[:], in0=nlam[:], scalar1=-lam_init0, scalar2=None, op0=ALU.add), reads=[nlam], writes=[nlam])
            S.op("dve", lambda e: e.tensor_scalar(out=sg_bc[:], in0=sg_bc[:], scalar1=1.0 - lam_init0, scalar2=None, op0=ALU.mult), reads=[sg_bc], writes=[sg_bc])
            S.dma("sp", lambda e: e.dma_start(out=SG_d[:], in_=sg_bc[:]), reads=[sg_bc], writes=[SG_d])
            S.dma("sp", lambda e: e.dma_start(out=NL_d[:], in_=nlam[:]), reads=[nlam], writes=[NL_d])

            ropeb = S.sb("sa_rope", [NSEQ, 128], F32)
            S.dma("sp", lambda e: e.dma_start(out=ropeb[:], in_=ropeSA), writes=[ropeb])
            load_x_sample(b, xs_b, xs)
            q32 = S.sb("sa_q32", [NSEQ, 512], F32); k32 = S.sb("sa_k32", [NSEQ, 512], F32); v32 = S.sb("sa_v32", [NSEQ, 512], F32)
            proj_tm(pj, xT, W, 8, 0, 512, P)
            S.op("act", lambda e: e.activation(out=praw[:], in_=pj[0:P, :], func=AF.Copy), reads=[pj], writes=[praw])
            rope_rows(praw, q32, ropeb, 0, 8, 32, P, tmps)
            proj_tm(pj, xT, W, 8, 512, 512, P)
            S.op("act", lambda e: e.activation(out=praw[:], in_=pj[0:P, :], func=AF.Copy), reads=[pj], writes=[praw])
            rope_rows(praw, k32, ropeb, 64, 8, 32, P, tmps)
            S.dma("sp", lambda e: e.dma_start(out=ak_s, in_=k32[:]), reads=[k32])
            proj_tm(pj, xT, W, 8, 1024, 512, P)
            S.op("act", lambda e: e.activation(out=v32[:], in_=pj[0:P, :], func=AF.Copy), reads=[pj], writes=[v32])
            S.dma("sp", lambda e: e.dma_start(out=av_s, in_=v32[:]), reads=[v32])
            sq = S.sb("sa_sq", [NSEQ, 512], F32); ff = S.sb("sa_ff", [NSEQ, 512], F32); k1 = S.sb("sa_k1", [NSEQ, 512], F32); vi = S.sb("sa_vi", [NSEQ, 512], F32)
            gate = b["gate"]
            proj_tm(pj, xT, W, 8, 1536, 512, P)
            S.op("act", lambda e: e.activation(out=sq[:], in_=pj[0:P, :], func=AF.Silu), reads=[pj], writes=[sq])
            proj_tm(pj, xT, W, 8, 2048, 512, P)
            S.op("act", lambda e: e.activation(out=ff[:], in_=pj[0:P, :], func=AF.Sigmoid), reads=[pj], writes=[ff])
            S.op("dve", lambda e: e.tensor_tensor(out=ff[:], in0=ff[:], in1=oml_bc[0:P, :], op=ALU.mult), reads=[ff, oml_bc], writes=[ff])
            S.op("dve", lambda e: e.tensor_tensor(out=ff[:], in0=ff[:], in1=lb_bc[0:P, :], op=ALU.add), reads=[ff, lb_bc], writes=[ff])
            S.op("dve", lambda e: e.tensor_scalar(out=k1[:], in0=ff[:], scalar1=-1.0, scalar2=1.0, op0=ALU.mult, op1=ALU.add), reads=[ff], writes=[k1])
            proj_tm(pj, xT, W, 8, 2560, 512, P)
            S.op("act", lambda e: e.activation(out=vi[:], in_=pj[0:P, :], func=AF.Copy), reads=[pj], writes=[vi])
            proj_tm(pj, xT, W, 8, 3072, 512, P)
            S.op("act", lambda e: e.activation(out=gate[:], in_=pj[0:P, :], func=AF.Silu), reads=[pj], writes=[gate])
            names = [q32, k32, v32, sq, ff, k1, vi, gate]
            for n_, t_ in enumerate(names):
                S.dma("sp", lambda e, n_=n_, t_=t_: e.dma_start(out=SP_d[n_], in_=t_[:]), reads=[t_], writes=[SP_d])
        with S.phase():
            b = sample_common("sb_")
            tmps = b["tmps"]
            bng_bc = S.sb("sb_bng", [128, 128], F32); sg_bc = S.sb("sb_sg", [128, 128], F32); nlam = S.sb("sb_nlam", [128, 1], F32)
            bcast_load(bng_bc, ev_bng, 128)
            S.dma("sp", lambda e: e.dma_start(out=sg_bc[:], in_=SG_d[:]), reads=[SG_d], writes=[sg_bc])
            S.dma("sp", lambda e: e.dma_start(out=nlam[:], in_=NL_d[:]), reads=[NL_d], writes=[nlam])
            coef8 = S.sb("sb_coef8", [8, 1], F32); selc = S.sb("sb_selc", [8, 16, 16], F32)
            S.op("dve", lambda e: e.tensor_scalar(out=coef8[:], in0=nlam[0:8, :], scalar1=-1.0, scalar2=None, op0=ALU.add), reads=[nlam], writes=[coef8])
            S.op("dve", lambda e: e.tensor_tensor(out=coef8[:], in0=coef8[:], in1=hi4[:], op=ALU.mult), reads=[coef8, hi4], writes=[coef8])
            S.op("dve", lambda e: e.tensor_scalar(out=coef8[:], in0=coef8[:], scalar1=1.0, scalar2=None, op0=ALU.add), reads=[coef8], writes=[coef8])
            S.op("dve", lambda e: e.tensor_scalar(out=selc[:], in0=SelS[:], scalar1=coef8[:, 0:1], scalar2=None, op0=ALU.mult), reads=[SelS, coef8], writes=[selc])
            names = []
            for n_ in range(8):
                t_ = b["gate"] if n_ == 7 else S.sb("sb_l%d" % n_, [NSEQ, 512], F32)
                S.dma("sp", lambda e, n_=n_, t_=t_: e.dma_start(out=t_[:], in_=SP_d[n_]), reads=[SP_d], writes=[t_])
                names.append(t_)
            q32, k32, v32, sq, ff, k1, vi, gate = names
            qTf = S.sb("sa_qTf", [128, 4, NSEQ], F32); fTf = S.sb("sa_fTf", [128, 4, NSEQ], F32); kTf = S.sb("sa_kTf", [128, 4, NSEQ], F32)
            to_fm(sq, qTf, b["tpf"]); to_fm(ff, fTf, b["tpf"]); to_fm(k1, kTf, b["tpf"])
            state_steps(st_b, sb_s, vi, qTf, Col(lambda h, i: kTf[:, h, i:i + 1], [kTf]), Col(lambda h, i: fTf[:, h, i:i + 1], [fTf]),
                        b["pO"], b["pvb"], b["Sb"], b["tmpb"], b["Qz"])
            on32, sqq, ss, rs, omb = b["on32"], b["sqq"], b["ss"], b["rs"], b["omb"]
            head_norm(b["pO"], b["pO"][0:P, :].rearrange("p (h e) -> p h e", h=4), on32, P, sqq, ss, rs, center=False)
            S.op("pool", lambda e: e.tensor_tensor(out=on32[:], in0=on32[:], in1=bng_bc[0:P, :].unsqueeze(1).to_broadcast([P, 4, 128]), op=ALU.mult), reads=[on32, bng_bc], writes=[on32])
            S.op("dve", lambda e: e.tensor_tensor(out=omb[:, 512:1024], in0=on32[:].rearrange("p h e -> p (h e)"), in1=gate[:], op=ALU.mult), reads=[on32, gate], writes=[omb])
            idxf = S.sb("sa_idxf", [128, NSEQ * NPG], F32); idxi = S.sb("sa_idxi", [128, NSEQ * NPG], I32); pti = S.sb("sa_pti", [128, NSEQ * NPG], I32)
            iot = S.sb("sa_iot", [128, 1], I32); iotf = S.sb("sa_iotf", [128, 1], F32)
            S.dma("sp", lambda e: e.dma_start(out=pti[:], in_=ptab.partition_broadcast(128)), writes=[pti])
            S.op("pool", lambda e: e.iota(out=iot[:], pattern=[[0, 1]], base=0, channel_multiplier=1), writes=[iot])
            S.op("dve", lambda e: e.tensor_copy(out=iotf[:], in_=iot[:]), reads=[iot], writes=[iotf])
            S.op("dve", lambda e: e.tensor_copy(out=idxf[:], in_=pti[:]), reads=[pti], writes=[idxf])
            S.op("dve", lambda e: e.tensor_scalar(out=idxf[:], in0=idxf[:], scalar1=128.0, scalar2=iotf[:, 0:1], op0=ALU.mult, op1=ALU.add), reads=[idxf, iotf], writes=[idxf])
            S.op("dve", lambda e: e.tensor_copy(out=idxi[:], in_=idxf[:]), reads=[idxf], writes=[idxi])
            NP1 = NPG + 1
            Kb = S.sb("sa_Kb", [128, NP1, 512], F32); Vb = S.sb("sa_Vb", [128, NP1, 512], F32)
            mb = S.sb("sa_mb", [8, 128], F32)
            S.op("pool", lambda e: e.memset(mb[:], 0.0), writes=[mb])
            S.op("pool", lambda e: e.memset(mb[:, 1:128], -30000.0), reads=[mb], writes=[mb])
            S.op("pool", lambda e: e.memset(Vb[:, NPG, :], 0.0), writes=[Vb])
            scb = S.sb("sa_scb", [128, NP1, 8], F32); scT = S.sb("sa_scT", [8, NP1 * 128], F32); PTb = S.sb("sa_PTb", [128, NP1, 8], F32)
            outm = S.sb("sa_outm", [8, 512], F32); mx = S.sb("sa_mx", [8, 1], F32); ll = S.sb("sa_ll", [8, 1], F32)
            psT = [S.ps("sa_psT%d" % i_, [128, 512], F32) for i_ in range(2)]
            pAcc = S.ps("sa_pAcc", [128, 512], F32)
            pqb, pPT, pOut = b["pvb"], b["pj"], b["tpf"]
            pOut = psT[0]
            for i in range(NSEQ):
                for j in range(NPG):
                    col = i * NPG + j
                    S.dma("pool", lambda e, j=j, col=col: e.indirect_dma_start(out=Kb[:, j, :], out_offset=None, in_=kpool,
                                                                               in_offset=bass.IndirectOffsetOnAxis(ap=idxi[:, col:col + 1], axis=0)), reads=[idxi], writes=[Kb])
                    S.dma("pool", lambda e, j=j, col=col: e.indirect_dma_start(out=Vb[:, j, :], out_offset=None, in_=vpool,
                                                                               in_offset=bass.IndirectOffsetOnAxis(ap=idxi[:, col:col + 1], axis=0)), reads=[idxi], writes=[Vb])
                S.op("pool", lambda e: e.memset(Kb[:, NPG, :], 0.0), reads=[Kb], writes=[Kb])
                S.dma("sp", lambda e, i=i: e.dma_start(out=Kb[0:1, NPG, :], in_=k32[i:i + 1, :]), reads=[k32, Kb], writes=[Kb])
                S.dma("sp", lambda e, i=i: e.dma_start(out=Vb[0:1, NPG, :], in_=v32[i:i + 1, :]), reads=[v32, Vb], writes=[Vb])
                dec_attend(i, q32, 512, 8, 64, Kb, Vb, NP1, True, mb, selc, bm8, pqb, psT, pPT, pOut, pAcc, scb, scT, PTb, outm, mx, ll)
            oa = S.sb("sa_oa", [NSEQ, 4, 128], F32)
            S.op("act", lambda e: e.activation(out=oa[:].rearrange("p h e -> p (h e)"), in_=pAcc[0:P, :], func=AF.Copy), reads=[pAcc], writes=[oa])
            head_norm(oa, oa[:, :, :], on32, P, sqq, ss, rs, center=False)
            S.op("dve", lambda e: e.tensor_tensor(out=omb[:, 0:512].rearrange("p (h e) -> p h e", h=4), in0=on32[:], in1=sg_bc[0:P, :].unsqueeze(1).to_broadcast([P, 4, 128]), op=ALU.mult),
                 reads=[on32, sg_bc], writes=[omb])
            S.dma("sp", lambda e: e.dma_start(out=OMS_d[:], in_=omb[:]), reads=[omb], writes=[OMS_d])


    def phase_SC1(l, w_out_ap, xsrc_b, xsrc_ap):
        with S.phase():
            P = NSEQ
            Wo = S.sb("sWo", [128, 8, D], BF16); load_w_bf16(Wo, w_out_ap, D, D)
            Wq = S.sb("sWq", [128, 8, D], BF16); load_w_bf16(Wq, xa_wq[l], D, D)
            Wx = S.sb("sWx", [128, 8, D], BF16); load_w_bf16(Wx, xa_wo[l], D, D)
            g1 = S.sb("sg1", [128, D], F32); b1 = S.sb("sb1", [128, D], F32); g2 = S.sb("sg2", [128, D], F32); b2 = S.sb("sb2", [128, D], F32)
            bcast_load(g1, ln_g[3 * l:3 * l + 1, :], D); bcast_load(b1, ln_b[3 * l:3 * l + 1, :], D)
            bcast_load(g2, ln_g[3 * l + 1:3 * l + 2, :], D); bcast_load(b2, ln_b[3 * l + 1:3 * l + 2, :], D)
            xin = S.sb("s_xin", [P, D], F32); om = S.sb("s_om", [P, D], BF16); oT = S.sb("s_oT", [128, 8, P], BF16)
            zz = S.sb("s_z", [P, D], F32); x1 = S.sb("s_x1", [P, D], F32); x1b = S.sb("s_x1b", [P, D], BF16); q32s = S.sb("s_q32", [P, D], F32)
            Kc = S.sb("s_Kc", [128, 2, D], F32); Vc = S.sb("s_Vc", [128, 2, D], F32)
            scb = S.sb("s_scb", [128, 2, 4], F32); scT = S.sb("s_scT", [4, 256], F32); PTb = S.sb("s_PTb", [128, 2, 4], F32)
            outm = S.sb("s_outm", [4, D], F32); mx = S.sb("s_mx", [8, 1], F32); ll = S.sb("s_ll", [8, 1], F32)
            st = S.sb("s_st", [P, 2, 6], F32); mv_ = S.sb("s_mv", [P, 2], F32); rs = S.sb("s_rs", [P, 1], F32)
            tp = S.ps("s_tp", [128, 8, 128], BF16)
            pj = S.ps("s_pj", [128, 512], F32)
            pqb = S.ps("s_pqb", [128, D], F32); pAcc = S.ps("s_pAcc", [128, D], F32)
            psT = [S.ps("s_psT", [128, 512], F32)]; pPT = S.ps("s_pPT", [128, 512], F32)
            S.dma("sp", lambda e: e.dma_start(out=xin[:], in_=xsrc_ap), reads=[xsrc_b], writes=[xin])
            S.dma("sp", lambda e: e.dma_start(out=om[:], in_=OMS_d[:]), reads=[OMS_d], writes=[om])
            transposes(om, 8, P, tp, oT)
            for g in range(2):
                proj_tm(pj, oT, Wo, 8, g * 512, 512, P)
                S.op("dve", lambda e, g=g: e.scalar_tensor_tensor(out=zz[:, g * 512:(g + 1) * 512], in0=xin[:, g * 512:(g + 1) * 512], scalar=ALPHA, in1=pj[0:P, :], op0=ALU.mult, op1=ALU.add),
                     reads=[xin, pj], writes=[zz])
            layernorm(zz, x1, P, g1, b1, st, mv_, rs)
            S.op("act", lambda e: e.activation(out=x1b[:], in_=x1[:], func=AF.Copy), reads=[x1], writes=[x1b])
            transposes(x1b, 8, P, tp, oT)
            for g in range(2):
                proj_tm(pj, oT, Wq, 8, g * 512, 512, P)
                S.op("act", lambda e, g=g: e.activation(out=q32s[:, g * 512:(g + 1) * 512], in_=pj[0:P, :], func=AF.Copy, scale=1.0 / 16), reads=[pj], writes=[q32s])
            for i in range(NSEQ):
                S.dma("sp", lambda e, i=i: e.dma_start(out=Kc[:], in_=cmk[l, i].rearrange("(t p) f -> p t f", p=128)), writes=[Kc])
                S.dma("sp", lambda e, i=i: e.dma_start(out=Vc[:], in_=cmv[l, i].rearrange("(t p) f -> p t f", p=128)), writes=[Vc])
                dec_attend(i, q32s, D, 4, 256, Kc, Vc, 2, False, None, SelS, bm4, pqb, psT, pPT, pqb, pAcc, scb, scT, PTb, outm, mx, ll)
            S.op("act", lambda e: e.activation(out=om[:], in_=pAcc[0:P, :], func=AF.Copy), reads=[pAcc], writes=[om])
            transposes(om, 8, P, tp, oT)
            for g in range(2):
                proj_tm(pj, oT, Wx, 8, g * 512, 512, P)
                S.op("dve", lambda e, g=g: e.scalar_tensor_tensor(out=zz[:, g * 512:(g + 1) * 512], in0=x1[:, g * 512:(g + 1) * 512], scalar=ALPHA, in1=pj[0:P, :], op0=ALU.mult, op1=ALU.add),
                     reads=[x1, pj], writes=[zz])
            layernorm(zz, xin, P, g2, b2, st, mv_, rs)
            S.dma("sp", lambda e: e.dma_start(out=XS2_d[:], in_=xin[:]), reads=[xin], writes=[XS2_d])

    def phase_SC2(l, xdst_b, xdst_ap):
        with S.phase():
            P = NSEQ
            Wu = S.sb("sWu", [128, 8, 2 * DFF], BF16); load_w_bf16(Wu, ffn_w_up[l], D, 2 * DFF)
            Wd = S.sb("sWd", [128, 22, D], BF16); load_w_bf16(Wd, ffn_w_down[l], DFF, D)
            g3 = S.sb("sg3", [128, D], F32); b3 = S.sb("sb3", [128, D], F32)
            bcast_load(g3, ln_g[3 * l + 2:3 * l + 3, :], D); bcast_load(b3, ln_b[3 * l + 2:3 * l + 3, :], D)
            cw = S.sb("scw", [128, 3, 22], F32); cb = S.sb("scb_", [128, 22], F32)
            S.dma("sp", lambda e: e.dma_start(out=cw[:], in_=ffn_cw[l].rearrange("i (j p) -> p i j", p=128), allow_slow_non_contiguous=True), writes=[cw])
            S.dma("sp", lambda e: e.dma_start(out=cb[:], in_=ffn_cb[l].rearrange("(j p) -> p j", p=128), allow_slow_non_contiguous=True), writes=[cb])
            x2 = S.sb("sf_x2", [P, D], F32); x2b = S.sb("sf_x2b", [P, D], BF16); x2T = S.sb("sf_x2T", [128, 8, P], BF16)
            bufTM = S.sb("sf_bufTM", [2 * P, DFF], F32); bufT = S.sb("sf_bufT", [128, 22, 2 * P], F32)
            upT = S.sb("sf_upT", [128, 22, P], F32); cv = S.sb("sf_cv", [128, P], F32); ge = S.sb("sf_ge", [128, P], F32)
            hT = S.sb("sf_hT", [128, 22, P], BF16); zz = S.sb("sf_z", [P, D], F32); upo = S.sb("sf_upo", [P, DFF], F32)
            st = S.sb("sf_st", [P, 2, 6], F32); mv_ = S.sb("sf_mv", [P, 2], F32); rs = S.sb("sf_rs", [P, 1], F32)
            tp = S.ps("sf_tp", [128, 8, 128], BF16)
            pu = [S.ps("sf_pu%d" % i, [128, P], F32) for i in range(2)]; pg = [S.ps("sf_pg%d" % i, [128, P], F32) for i in range(2)]
            pj = [S.ps("sf_pj%d" % i, [128, 512], F32) for i in range(2)]
            S.dma("sp", lambda e: e.dma_start(out=x2[:], in_=XS2_d[:]), reads=[XS2_d], writes=[x2])
            S.dma("sp", lambda e: e.dma_start(out=bufTM[:], in_=sconv[l]), writes=[bufTM])
            S.dma("sp", lambda e: e.dma_start(out=cv_s[l][:, 0, :], in_=sconv[l].rearrange("(i t) f -> i t f", t=2)[:, 1, :]))
            S.op("act", lambda e: e.activation(out=x2b[:], in_=x2[:], func=AF.Copy), reads=[x2], writes=[x2b])
            transposes(x2b, 8, P, tp, x2T)
            for j0 in range(0, 22, 11):
                for j in range(j0, j0 + 11):
                    S.op("pe", lambda e, j=j, j0=j0: e.transpose(out=pj[0][:, (j - j0) * 32:(j - j0 + 1) * 32], in_=bufTM[:, j * 128:(j + 1) * 128], identity=identf[0:32, 0:32]),
                         reads=[bufTM, identf], writes=[pj[0]])
                S.op("dve", lambda e, j0=j0: e.tensor_copy(out=bufT[:, j0:j0 + 11, :], in_=pj[0][:, 0:352].rearrange("p (j c) -> p j c", c=32)), reads=[pj[0]], writes=[bufT])
            for j in range(22):
                pu_, pg_ = pu[j % 2], pg[j % 2]
                for k in range(8):
                    S.op("pe", lambda e, j=j, k=k, pu_=pu_: e.matmul(pu_[:], lhsT=Wu[:, k, j * 128:(j + 1) * 128], rhs=x2T[:, k, :], start=(k == 0), stop=(k == 7)), reads=[Wu, x2T], writes=[pu_])
                for k in range(8):
                    S.op("pe", lambda e, j=j, k=k, pg_=pg_: e.matmul(pg_[:], lhsT=Wu[:, k, DFF + j * 128:DFF + (j + 1) * 128], rhs=x2T[:, k, :], start=(k == 0), stop=(k == 7)), reads=[Wu, x2T], writes=[pg_])
                S.op("act", lambda e, j=j, pu_=pu_: e.activation(out=upT[:, j, :], in_=pu_[:], func=AF.Copy), reads=[pu_], writes=[upT])
                tv = bufT[:, j, :].rearrange("p (i t) -> p i t", t=2)
                S.op("dve", lambda e, j=j: e.tensor_scalar(out=cv[:], in0=upT[:, j, :], scalar1=cw[:, 2, j:j + 1], scalar2=cb[:, j:j + 1], op0=ALU.mult, op1=ALU.add), reads=[upT, cw, cb], writes=[cv])
                S.op("dve", lambda e, j=j, tv=tv: e.scalar_tensor_tensor(out=cv[:], in0=tv[:, :, 1], scalar=cw[:, 1, j:j + 1], in1=cv[:], op0=ALU.mult, op1=ALU.add), reads=[bufT, cw, cv], writes=[cv])
                S.op("dve", lambda e, j=j, tv=tv: e.scalar_tensor_tensor(out=cv[:], in0=tv[:, :, 0], scalar=cw[:, 0, j:j + 1], in1=cv[:], op0=ALU.mult, op1=ALU.add), reads=[bufT, cw, cv], writes=[cv])
                S.op("act", lambda e: e.activation(out=ge[:], in_=cv[:], func=AF.Gelu_apprx_tanh), reads=[cv], writes=[ge])
                S.op("dve", lambda e, j=j, pg_=pg_: e.tensor_tensor(out=hT[:, j, :], in0=ge[:], in1=pg_[:], op=ALU.mult), reads=[ge, pg_], writes=[hT])
            for g in range(2):
                for j in range(22):
                    S.op("pe", lambda e, g=g, j=j: e.matmul(pj[g][0:P, :], lhsT=hT[:, j, :], rhs=Wd[:, j, g * 512:(g + 1) * 512], start=(j == 0), stop=(j == 21)), reads=[hT, Wd], writes=[pj[g]])
                S.op("dve", lambda e, g=g: e.scalar_tensor_tensor(out=zz[:, g * 512:(g + 1) * 512], in0=x2[:, g * 512:(g + 1) * 512], scalar=ALPHA, in1=pj[g][0:P, :], op0=ALU.mult, op1=ALU.add),
                     reads=[x2, pj[g]], writes=[zz])
            layernorm(zz, x2, P, g3, b3, st, mv_, rs)
            S.dma("sp", lambda e: e.dma_start(out=xdst_ap, in_=x2[:]), reads=[x2], writes=[xdst_b])
            for j in range(22):
                S.op("pe", lambda e, j=j: e.transpose(out=pj[0][0:P, (j % 4) * 128:(j % 4 + 1) * 128], in_=upT[:, j, :], identity=identf[:]), reads=[upT, identf], writes=[pj[0]])
                if j % 4 == 3 or j == 21:
                    j0 = (j // 4) * 4
                    S.op("dve", lambda e, j=j, j0=j0: e.tensor_copy(out=upo[:, j0 * 128:(j + 1) * 128], in_=pj[0][0:P, 0:(j - j0 + 1) * 128]), reads=[pj[0]], writes=[upo])
            S.dma("sp", lambda e: e.dma_start(out=cv_s[l][:, 1, :], in_=upo[:]), reads=[upo])

    def phase_SA1():
        with S.phase():
            P = NSEQ
            W = S.sb("W1s", [128, 8, 4104], BF16)
            load_w_bf16(W, od_w_in, D, 4104)
            b = sample_common("sc_")
            xT, pj, praw, tmps = b["xT"], b["pj"], b["praw"], b["tmps"]
            ropeb = S.sb("sc_rope", [NSEQ, 256], F32); bif_bc = S.sb("sc_bif", [128, 8], F32)
            S.dma("sp", lambda e: e.dma_start(out=ropeb[:], in_=ropeSC), writes=[ropeb])
            bcast_load(bif_bc, od_b_if, 8)
            load_x_sample(b, XS_d, XS_d.h[:])
            outs = [S.sb("sc_o%d" % i, [NSEQ, 512], F32) for i in range(9)]
            for gi in range(8):
                proj_tm(pj, xT, W, 8, gi * 512, 512, P)
                o_ = outs[gi]
                if gi in (0, 1):
                    S.op("act", lambda e: e.activation(out=praw[:], in_=pj[0:P, :], func=AF.Copy), reads=[pj], writes=[praw])
                    rope_rows(praw, o_, ropeb, 0, 4, 64, P, tmps)
                    if gi == 1:
                        S.op("dve", lambda e, o_=o_: e.tensor_scalar(out=o_[:], in0=o_[:], scalar1=128.0 ** -0.5, scalar2=None, op0=ALU.mult), reads=[o_], writes=[o_])
                elif gi == 3:
                    S.op("act", lambda e, o_=o_: e.activation(out=o_[:], in_=pj[0:P, :], func=AF.Silu), reads=[pj], writes=[o_])
                elif gi == 7:
                    S.op("act", lambda e, o_=o_: e.activation(out=o_[:], in_=pj[0:P, :], func=AF.Sigmoid), reads=[pj], writes=[o_])
                elif gi == 4:
                    S.op("act", lambda e, o_=o_: e.activation(out=o_[:], in_=pj[0:P, :], func=AF.Copy, scale=128.0 ** -0.5), reads=[pj], writes=[o_])
                else:
                    S.op("act", lambda e, o_=o_: e.activation(out=o_[:], in_=pj[0:P, :], func=AF.Copy), reads=[pj], writes=[o_])
            proj_tm(pj, xT, W, 8, 4096, 8, P)
            S.op("pool", lambda e: e.memset(outs[8][:], 0.0), writes=[outs[8]])
            S.op("dve", lambda e: e.tensor_tensor(out=outs[8][:, 0:8], in0=pj[0:P, 0:8], in1=bif_bc[0:P, :], op=ALU.add), reads=[pj, bif_bc, outs[8]], writes=[outs[8]])
            for n_, t_ in enumerate(outs):
                S.dma("sp", lambda e, n_=n_, t_=t_: e.dma_start(out=SP1_d[n_], in_=t_[:]), reads=[t_], writes=[SP1_d])
        with S.phase():
            P = NSEQ
            b = sample_common("sd_")
            cng_bc = S.sb("sd_cng", [128, 128], F32); dng_bc = S.sb("sd_dng", [128, 128], F32)
            bcast_load(cng_bc, od_cng, 128); bcast_load(dng_bc, od_dng, 128)
            ins = []
            for n_ in range(9):
                t_ = S.sb("sd_l%d" % n_, [NSEQ, 512], F32)
                S.dma("sp", lambda e, n_=n_, t_=t_: e.dma_start(out=t_[:], in_=SP1_d[n_]), reads=[SP1_d], writes=[t_])
                ins.append(t_)
            qc, kc, vc, gc, qd, kd, vd, gd, gif = ins
            on32, sqq, ss, rs, omb, pO, tpf = b["on32"], b["sqq"], b["ss"], b["rs"], b["omb"], b["pO"], b["tpf"]
            qTf = S.sb("sd_qTf", [128, 4, NSEQ], F32); kTf = S.sb("sd_kTf", [128, 4, NSEQ], F32)
            to_fm(qc, qTf, tpf); to_fm(kc, kTf, tpf)
            state_steps(st_c, sc_s, vc, qTf, Col(lambda h, i: kTf[:, h, i:i + 1], [kTf]), Col(lambda h, i: GAMMA[h], []), pO, b["pvb"], b["Sb"], b["tmpb"], b["Qz"])
            head_norm(pO, pO[0:P, :].rearrange("p (h e) -> p h e", h=4), on32, P, sqq, ss, rs, center=True)
            S.op("pool", lambda e: e.tensor_tensor(out=on32[:], in0=on32[:], in1=cng_bc[0:P, :].unsqueeze(1).to_broadcast([P, 4, 128]), op=ALU.mult), reads=[on32, cng_bc], writes=[on32])
            S.op("dve", lambda e: e.tensor_tensor(out=omb[:, 0:512], in0=on32[:].rearrange("p h e -> p (h e)"), in1=gc[:], op=ALU.mult), reads=[on32, gc], writes=[omb])
            lf = S.sb("sd_lf", [P, 4], F32); mp = S.sb("sd_mp", [P, 4], F32); t1_ = S.sb("sd_t1", [P, 4], F32); mn = S.sb("sd_mn", [P, 4], F32)
            inter = S.sb("sd_inter", [P, 4], F32); wv = S.sb("sd_w", [P, 4], F32); enm = S.sb("sd_enm", [P, 4], F32)
            S.dma("sp", lambda e: e.dma_start(out=mp[:], in_=st_dm), writes=[mp])
            S.op("act", lambda e: e.activation(out=lf[:], in_=gif[:, 4:8], func=AF.Sigmoid), reads=[gif], writes=[lf])
            S.op("act", lambda e: e.activation(out=lf[:], in_=lf[:], func=AF.Ln), reads=[lf], writes=[lf])
            S.op("dve", lambda e: e.tensor_tensor(out=t1_[:], in0=lf[:], in1=mp[:], op=ALU.add), reads=[lf, mp], writes=[t1_])
            S.op("dve", lambda e: e.tensor_tensor(out=mn[:], in0=t1_[:], in1=gif[:, 0:4], op=ALU.max), reads=[t1_, gif], writes=[mn])
            S.dma("sp", lambda e: e.dma_start(out=dm_s, in_=mn[:]), reads=[mn])
            S.op("dve", lambda e: e.tensor_tensor(out=t1_[:], in0=t1_[:], in1=mn[:], op=ALU.subtract), reads=[t1_, mn], writes=[t1_])
            S.op("act", lambda e: e.activation(out=inter[:], in_=t1_[:], func=AF.Exp), reads=[t1_], writes=[inter])
            S.op("dve", lambda e: e.tensor_tensor(out=t1_[:], in0=gif[:, 0:4], in1=mn[:], op=ALU.subtract), reads=[gif, mn, inter], writes=[t1_])
            S.op("act", lambda e: e.activation(out=wv[:], in_=t1_[:], func=AF.Exp), reads=[t1_], writes=[wv])
            S.op("act", lambda e: e.activation(out=enm[:], in_=mn[:], func=AF.Exp, scale=-1.0), reads=[mn], writes=[enm])
            X = S.sb("sd_X", [16, 4, 16], F32); inter_bc = S.sb("sd_ibc", [128, 4, NSEQ], F32); w_bc = S.sb("sd_wbc", [128, 4, NSEQ], F32)
            bcast_rows_fm(inter, inter_bc, X, b["pj"]); bcast_rows_fm(wv, w_bc, X, b["pj"])
            kwT = S.sb("sd_kwT", [128, 4, NSEQ], F32); nTM = S.sb("sd_nTM", [P, 512], F32); nT = S.sb("sd_nT", [128, 4, NSEQ], F32)
            to_fm(kd, kTf, tpf)
            S.op("dve", lambda e: e.tensor_tensor(out=kwT[:], in0=kTf[:], in1=w_bc[:], op=ALU.mult), reads=[kTf, w_bc], writes=[kwT])
            S.dma("sp", lambda e: e.dma_start(out=nTM[:], in_=st_dn), writes=[nTM])
            to_fm(nTM, nT, tpf)
            S.op("dve", lambda e: e.tensor_tensor(out=nT[:], in0=nT[:], in1=inter_bc[:], op=ALU.mult), reads=[nT, inter_bc], writes=[nT])
            S.op("dve", lambda e: e.tensor_tensor(out=nT[:], in0=nT[:], in1=kwT[:], op=ALU.add), reads=[nT, kwT], writes=[nT])
            for h in range(4):
                S.op("pe", lambda e, h=h: e.transpose(out=b["pj"][0:P, h * 128:(h + 1) * 128], in_=nT[:, h, :], identity=identf[:]), reads=[nT, identf], writes=[b["pj"]])
            S.op("act", lambda e: e.activation(out=nTM[:], in_=b["pj"][0:P, :], func=AF.Copy), reads=[b["pj"]], writes=[nTM])
            S.dma("sp", lambda e: e.dma_start(out=dn_s, in_=nTM[:]), reads=[nTM])
            to_fm(qd, qTf, tpf)
            state_steps(st_dc, dc_s, vd, qTf, Col(lambda h, i: kwT[:, h, i:i + 1], [kwT]), Col(lambda h, i: inter_bc[:, h, i:i + 1], [inter_bc]), pO, b["pvb"], b["Sb"], b["tmpb"], b["Qz"])
            prod = S.sb("sd_prod", [128, 4, NSEQ], F32); den = S.sb("sd_den", [P, 4], F32); hh = S.sb("sd_hh", [P, 4, 128], F32)
            S.op("dve", lambda e: e.tensor_tensor(out=prod[:], in0=qTf[:], in1=nT[:], op=ALU.mult), reads=[qTf, nT], writes=[prod])
            for h in range(4):
                S.op("pe", lambda e, h=h: e.matmul(b["pj"][0:P, 2 * h:2 * h + 2], lhsT=prod[:, h, :], rhs=onesf[:, 0:2], start=True, stop=True), reads=[prod, onesf], writes=[b["pj"]])
            S.op("dve", lambda e: e.tensor_copy(out=den[:], in_=b["pj"][0:P, 0:8].rearrange("p (h t) -> p h t", t=2)[:, :, 0]), reads=[b["pj"]], writes=[den])
            S.op("dve", lambda e: e.tensor_scalar(out=mn[:], in0=den[:], scalar1=-1.0, scalar2=None, op0=ALU.mult), reads=[den], writes=[mn])
            S.op("dve", lambda e: e.tensor_tensor(out=den[:], in0=den[:], in1=mn[:], op=ALU.max), reads=[den, mn], writes=[den])
            S.op("dve", lambda e: e.tensor_tensor(out=den[:], in0=den[:], in1=enm[:], op=ALU.max), reads=[den, enm], writes=[den])
            S.op("dve", lambda e: e.reciprocal(out=den[:], in_=den[:]), reads=[den], writes=[den])
            S.op("dve", lambda e: e.tensor_tensor(out=hh[:], in0=pO[0:P, :].rearrange("p (h e) -> p h e", h=4), in1=den[:].unsqueeze(2).to_broadcast([P, 4, 128]), op=ALU.mult),
                 reads=[pO, den], writes=[hh])
            head_norm(hh, hh[:, :, :], on32, P, sqq, ss, rs, center=True)
            S.op("pool", lambda e: e.tensor_tensor(out=on32[:], in0=on32[:], in1=dng_bc[0:P, :].unsqueeze(1).to_broadcast([P, 4, 128]), op=ALU.mult), reads=[on32, dng_bc], writes=[on32])
            S.op("dve", lambda e: e.tensor_tensor(out=omb[:, 512:1024], in0=on32[:].rearrange("p h e -> p (h e)"), in1=gd[:], op=ALU.mult), reads=[on32, gd], writes=[omb])
            S.dma("sp", lambda e: e.dma_start(out=OMS_d[:], in_=omb[:]), reads=[omb], writes=[OMS_d])

    if stop_after != "S":
        phase_A0()
    if stop_after == "A0":
        return nc, S
    if stop_after not in (None, "S"):
        phase_B0()
    if stop_after == "C0":
        phase_C1(0, ev_w_out, xp_b, xp)
        phase_C2(0, yp_b, y_p)
        return nc, S
    if stop_after == "SA0":
        phase_SA0()
        dbg = nc.dram_tensor("dbg_oms", [NSEQ, D], BF16, kind="ExternalOutput").ap()
        with S.phase():
            S.dma("sp", lambda e: e.dma_start(out=dbg, in_=OMS_d[:]), reads=[OMS_d])
        return nc, S
    if stop_after == "A1":
        phase_C1(0, ev_w_out, xp_b, xp)
        phase_C2(0, X_d, X_d.h)
        phase_A1()
        dbg = nc.dram_tensor("dbg_omix", [NT, D], BF16, kind="ExternalOutput").ap()
        with S.phase():
            S.dma("sp", lambda e: e.dma_start(out=dbg, in_=OMIX_d[:]), reads=[OMIX_d])
        return nc, S
    if stop_after is None or stop_after == "S":
        if stop_after is None:
            phase_B0()
            phase_C1(0, ev_w_out, xp_b, xp)
            phase_C2(0, X_d, X_d.h)
            phase_A1()
            phase_C1(1, od_w_out, X_d, X_d.h)
            phase_C2(1, yp_b, y_p)
        phase_SA0()
        phase_SC1(0, ev_w_out, xs_b, xs)
        phase_SC2(0, XS_d, XS_d.h[:])
        phase_SA1()
        phase_SC1(1, od_w_out, XS_d, XS_d.h[:])
        phase_SC2(1, ys_b, y_s)
        return nc, S
    if stop_after == "P":
        phase_C1(0, ev_w_out, xp_b, xp)
        phase_C2(0, X_d, X_d.h)
        phase_A1()
        phase_C1(1, od_w_out, X_d, X_d.h)
        phase_C2(1, yp_b, y_p)
        return nc, S
    if stop_after == "B0":
        dbg = nc.dram_tensor("dbg_omix", [NT, D], BF16, kind="ExternalOutput").ap()
        with S.phase():
            S.dma("sp", lambda e: e.dma_start(out=dbg, in_=OMIX_d[:]), reads=[OMIX_d])
        return nc, S
    return nc, S


def _rope_tab(pos, d, qscale, kscale):
    inv = 10000.0 ** (-np.arange(0, d // 2, dtype=np.float64) * 2.0 / d)
    ang = np.asarray(pos, np.float64)[:, None] * inv[None, :]
    c, s_ = np.cos(ang), np.sin(ang)
    return np.concatenate([c * qscale, s_ * qscale, c * kscale, s_ * kscale], 1).astype(np.float32)


def make_in_map(inp, NT, core):
    b = core % 2
    pos = np.arange(NT)
    t = np.arange(128) % 64
    g = np.array(GAMMA, np.float64)
    retg = np.concatenate([g[None, :] ** (t[:, None] + 1), (128.0 ** -0.5) * g[None, :] ** (-(t[:, None] + 1.0)),
                           (128.0 ** -0.5) * g[None, :] ** (63.0 - t[:, None])], 1).astype(np.float32)
    retdec = np.tile(np.repeat(np.tile(g ** 64, 2), 2)[None, :], (128, 1)).astype(np.float32)
    f = np.ascontiguousarray
    m = {
        "xp": f(inp["x_prompt"][b, :NT]),
        "mem": f(inp["mem_prompt"][b]),
        "ropeA": _rope_tab(pos, 64, 0.125, 1.0),
        "ropeC": _rope_tab(pos, 128, 1.0, 1.0),
        "retg": retg, "retdec": retdec,
        "ev_w_in": f(inp["ev_w_in"][0]), "ev_w_out": f(inp["ev_w_out"][0]),
        "ev_lam": f(inp["ev_lam"][0].reshape(1, 256)), "ev_subln_g": f(inp["ev_subln_g"][0].reshape(1, 128)),
        "ev_lb": f(inp["ev_lb_logits"].reshape(1, 1024)), "ev_bng": f(inp["ev_b_norm_g"][0].reshape(1, 128)),
        "od_w_in": f(inp["od_w_in"][0]), "od_b_if": f(inp["od_b_if"][0].reshape(1, 8)),
        "od_w_out": f(inp["od_w_out"][0]), "od_cng": f(inp["od_c_norm_g"][0].reshape(1, 128)),
        "od_dng": f(inp["od_d_norm_g"][0].reshape(1, 128)),
        "ln_g": f(inp["ln_g"].reshape(6, D)), "ln_b": f(inp["ln_b"].reshape(6, D)),
        "xa_wq": f(inp["xa_wq"]), "xa_wkv": f(inp["xa_wkv"]), "xa_wo": f(inp["xa_wo"]),
        "ffn_w_up": f(inp["ffn_w_up"]), "ffn_cw": f(inp["ffn_conv_w"]), "ffn_cb": f(inp["ffn_conv_b"]),
        "ffn_w_down": f(inp["ffn_w_down"]),
    }
    sl = slice(core * NSEQ, (core + 1) * NSEQ)
    pos_s = np.full((NSEQ,), inp["page_table"].shape[1] * inp["cache_a_k"].shape[2])
    m.update({
        "xs": f(inp["x_sample"][sl, 0]),
        "ropeSA": _rope_tab(pos_s, 64, 0.125, 1.0), "ropeSC": _rope_tab(pos_s, 128, 1.0, 1.0),
        "kpool": inp["cache_a_k"][0].reshape(-1, 512), "vpool": inp["cache_a_v"][0].reshape(-1, 512),
        "ptab": f(inp["page_table"][sl].reshape(1, -1)).astype(np.int32),
        "st_b": f(inp["state_b"][0, sl]), "st_c": f(inp["state_c"][0, sl]), "st_dc": f(inp["state_d_c"][0, sl]),
        "st_dn": f(inp["state_d_n"][0, sl].reshape(NSEQ, 512)), "st_dm": f(inp["state_d_m"][0, sl]),
        "cmk": f(inp["cache_mem_k"][:, sl].reshape(2, NSEQ, NMEM, D)), "cmv": f(inp["cache_mem_v"][:, sl].reshape(2, NSEQ, NMEM, D)),
        "sconv": f(inp["state_conv"][:, sl].reshape(2, NSEQ * 2, DFF)),
    })
    return {k: np.asarray(v) for k, v in m.items()}


_PROG = {}


def kernel(**inp):
    inp = {k: np.asarray(v) for k, v in inp.items()}
    NT = inp["x_prompt"].shape[1]
    NPOOL = inp["cache_a_k"].shape[1]
    key = (NT, NPOOL)
    if key not in _PROG:
        nc, S = build_program(NT, stop_after=None, NPOOL=NPOOL)
        S.finish()
        _PROG[key] = nc
    nc = _PROG[key]
    maps = [make_in_map(inp, NT, c) for c in range(8)]
    res = run_bass_kernel_spmd(nc, maps, core_ids=list(range(8)))
    r = res.results
    B, DB = 2, 8 * NSEQ
    f32 = np.float32

    def st(name, shape):
        return np.stack([np.asarray(r[b][name], f32).reshape(shape) for b in range(B)])

    def cat(name, shape):
        return np.concatenate([np.asarray(r[c][name], f32).reshape((NSEQ,) + shape) for c in range(8)], 0)

    y_prompt = st("y_p", (NT, D))
    y_sample = cat("y_s", (1, D))
    a_k_p = st("ak_p", (NT, 4, 2, 64))[None]
    a_v_p = st("av_p", (NT, 4, 128))[None]
    a_k_s = cat("ak_s", (1, 4, 2, 64))[None]
    a_v_s = cat("av_s", (1, 4, 128))[None]
    sb_p = st("sb_p", (4, 128, 128))[None]
    sb_s = cat("sb_s", (4, 128, 128))[None]
    sc_p = st("sc_p", (4, 128, 128))[None]
    sc_s = cat("sc_s", (4, 128, 128))[None]
    dc_p = st("dc_p", (4, 128, 128))[None]
    dc_s = cat("dc_s", (4, 128, 128))[None]
    dn_p = st("dn_p", (4, 128))[None]
    dn_s = cat("dn_s", (4, 128))[None]
    dm_p = st("dm_p", (4,))[None]
    dm_s = cat("dm_s", (4,))[None]
    mk_p = np.stack([np.asarray(r[b]["mk_p"], f32).reshape(2, NMEM, 4, 256) for b in range(B)], 1)
    mv_p = np.stack([np.asarray(r[b]["mv_p"], f32).reshape(2, NMEM, 4, 256) for b in range(B)], 1)
    cv_p = np.stack([np.asarray(r[b]["cv_p"], f32).reshape(2, 2, DFF) for b in range(B)], 1)
    cv_s = np.concatenate([np.asarray(r[c]["cv_s"], f32).reshape(2, NSEQ, 2, DFF) for c in range(8)], 1)
    return (y_prompt, y_sample, a_k_p, a_v_p, a_k_s, a_v_s, sb_p, sb_s, sc_p, sc_s, dc_p, dc_s,
            dn_p, dn_s, dm_p, dm_s, mk_p, mv_p, cv_p, cv_s)
```
